# Optimizing a Trainium2 kernel written in Bass

```python
import jax, jax.numpy as jnp
from jax import lax
import numpy as np

D_MODEL = 1024
BATCH = 16
SEQ = 4096
DEPTH = 1

MIX_WIDTH = D_MODEL
ATTN_HEAD_DIM = 64
ATTN_WIDTH = MIX_WIDTH // 2
ATTN_HEADS = ATTN_WIDTH // ATTN_HEAD_DIM
ATTN_KV_HEADS = ATTN_HEADS // 4
ATTN_KV_WIDTH = ATTN_KV_HEADS * ATTN_HEAD_DIM
WINDOW = 128
ATTN_BLOCK = 128
HGRN_KEY_DIM = 128
HGRN_WIDTH = MIX_WIDTH - ATTN_WIDTH
HGRN_HEADS = HGRN_WIDTH // HGRN_KEY_DIM
HGRN_VAL_DIM = HGRN_WIDTH // HGRN_HEADS
HGRN_KEY_WIDTH = HGRN_HEADS * HGRN_KEY_DIM
CHUNK = 64
IN_SPLITS = (ATTN_WIDTH, ATTN_KV_WIDTH, ATTN_KV_WIDTH, HGRN_KEY_WIDTH, HGRN_KEY_WIDTH, HGRN_WIDTH, HGRN_WIDTH)
IN_COLS = sum(IN_SPLITS)
IN_OFFSETS = tuple(int(s) for s in np.cumsum(IN_SPLITS)[:-1])
D_FF = ((8 * D_MODEL) // 3 + 255) // 256 * 256
CONV_WIDTH = 3
N_MOD = 6
EPS = 1e-6

kernel_name = "hymba_style_swa_sink_hgrn2_convffn_adaln"


def rms_norm(x, gain):
    xf = x.astype(jnp.float32)
    y = xf * lax.rsqrt(jnp.mean(xf * xf, axis=-1, keepdims=True) + EPS)
    return (y * gain.astype(jnp.float32)).astype(x.dtype)


def sliding_window_gqa_with_sinks(q, k, v, sinks):
    B, T = q.shape[0], q.shape[1]
    nb = T // ATTN_BLOCK
    G = ATTN_HEADS // ATTN_KV_HEADS
    qb = q.reshape(B, nb, ATTN_BLOCK, ATTN_KV_HEADS, G, ATTN_HEAD_DIM)

    def band_keys(t):
        prev = jnp.pad(t, ((0, 0), (ATTN_BLOCK, 0), (0, 0), (0, 0)))[:, :T]
        shp = (B, nb, ATTN_BLOCK, ATTN_KV_HEADS, ATTN_HEAD_DIM)
        return jnp.concatenate([prev.reshape(shp), t.reshape(shp)], axis=2)

    kb, vb = band_keys(k), band_keys(v)
    scores = jnp.einsum('bnqhgd,bnkhd->bnhgqk', qb.astype(jnp.float32), kb.astype(jnp.float32)) * (ATTN_HEAD_DIM ** -0.5)
    qi = jnp.arange(ATTN_BLOCK)[:, None]
    ki = jnp.arange(2 * ATTN_BLOCK)[None, :]
    rel = ATTN_BLOCK + qi - ki
    band = (rel >= 0) & (rel < WINDOW)
    not_pad = (jnp.arange(nb)[:, None, None] > 0) | (ki >= ATTN_BLOCK)[None]
    mask = band[None] & not_pad
    scores = jnp.where(mask[None, :, None, None], scores, -jnp.inf)
    sink = sinks.astype(jnp.float32).reshape(1, 1, ATTN_KV_HEADS, G, 1, 1)
    m = jnp.maximum(scores.max(axis=-1, keepdims=True), sink)
    e = jnp.exp(scores - m)
    p = e / (e.sum(axis=-1, keepdims=True) + jnp.exp(sink - m))
    out = jnp.einsum('bnhgqk,bnkhd->bnqhgd', p, vb.astype(jnp.float32))
    return out.reshape(B, T, ATTN_WIDTH).astype(q.dtype)


def hgrn2_chunkwise(q, f_logit, i, lower_bound):
    B, T, H, Dk = q.shape
    Dv = i.shape[-1]
    nc = T // CHUNK
    qf = jax.nn.silu(q.astype(jnp.float32))
    f = lower_bound + (1 - lower_bound) * jax.nn.sigmoid(f_logit.astype(jnp.float32))
    kf = 1 - f
    g = jnp.log(f)
    vf = i.astype(jnp.float32)

    def to_chunks(t):
        return t.reshape(B, nc, CHUNK, H, t.shape[-1]).transpose(1, 0, 3, 2, 4)

    qc, kc, vc = to_chunks(qf), to_chunks(kf), to_chunks(vf)
    bc = jnp.cumsum(to_chunks(g), axis=3)
    causal = jnp.tril(jnp.ones((CHUNK, CHUNK), dtype=bool))

    def step(S, inp):
        q_, k_, v_, b_ = inp
        b_last = b_[:, :, -1:, :]
        o_inter = jnp.einsum('bhtk,bhkv->bhtv', q_ * jnp.exp(b_), S)
        diff = b_[:, :, :, None, :] - b_[:, :, None, :, :]
        decay = jnp.exp(jnp.where(causal[:, :, None], diff, -jnp.inf))
        scores = jnp.einsum('bhtk,bhtsk,bhsk->bhts', q_, decay, k_)
        o_intra = jnp.einsum('bhts,bhsv->bhtv', scores, v_)
        S_new = jnp.exp(b_last[:, :, 0, :])[..., None] * S + jnp.einsum('bhsk,bhsv->bhkv', k_ * jnp.exp(b_last - b_), v_)
        return S_new, o_inter + o_intra

    S0 = jnp.zeros((B, H, Dk, Dv), jnp.float32)
    _, o = lax.scan(step, S0, (qc, kc, vc, bc))
    return o.transpose(1, 0, 3, 2, 4).reshape(B, T, H, Dv).astype(q.dtype)


def causal_depthwise_conv(u, w, b):
    out = lax.conv_general_dilated(
        u, w[:, None, :].astype(u.dtype), window_strides=(1,),
        padding=[(CONV_WIDTH - 1, 0)], dimension_numbers=('NWC', 'WIO', 'NWC'),
        feature_group_count=u.shape[-1])
    return out + b


def setup_inputs(seed: int = 0) -> dict:
    key = jax.random.key(seed)
    ks = jax.random.split(key, 18)
    f32 = jnp.float32

    def normal(k, shape, scale):
        return jax.random.normal(k, shape, f32) * scale

    def gain(k, shape):
        return 1.0 + 0.05 * jax.random.normal(k, shape, f32)

    return {
        "x": normal(ks[0], (BATCH, SEQ, D_MODEL), 1.0),
        "c": normal(ks[1], (BATCH, D_MODEL), 1.0),
        "ada_w": normal(ks[2], (DEPTH, D_MODEL, N_MOD * D_MODEL), 0.5 * D_MODEL ** -0.5),
        "ada_b": normal(ks[3], (DEPTH, N_MOD * D_MODEL), 0.02),
        "mix_norm_g": gain(ks[4], (DEPTH, D_MODEL)),
        "w_in": normal(ks[5], (DEPTH, D_MODEL, IN_COLS), D_MODEL ** -0.5),
        "b_in": normal(ks[6], (DEPTH, IN_COLS), 0.02),
        "attn_sinks": normal(ks[7], (DEPTH, ATTN_HEADS), 0.5),
        "attn_out_g": gain(ks[8], (DEPTH, ATTN_WIDTH)),
        "hgrn_lb_logits": normal(ks[9], (DEPTH + 1, HGRN_KEY_WIDTH), 0.5),
        "hgrn_out_g": gain(ks[10], (DEPTH, HGRN_WIDTH)),
        "w_out": normal(ks[11], (DEPTH, MIX_WIDTH, D_MODEL), MIX_WIDTH ** -0.5),
        "ffn_norm_g": gain(ks[12], (DEPTH, D_MODEL)),
        "w_up": normal(ks[13], (DEPTH, D_MODEL, 2 * D_FF), D_MODEL ** -0.5),
        "conv_w": normal(ks[14], (DEPTH, CONV_WIDTH, D_FF), CONV_WIDTH ** -0.5),
        "conv_b": normal(ks[15], (DEPTH, D_FF), 0.02),
        "w_down": normal(ks[16], (DEPTH, D_FF, D_MODEL), D_FF ** -0.5),
        "final_norm_g": gain(ks[17], (D_MODEL,)),
    }


def reference(x, c, ada_w, ada_b, mix_norm_g, w_in, b_in, attn_sinks, attn_out_g, hgrn_lb_logits,
              hgrn_out_g, w_out, ffn_norm_g, w_up, conv_w, conv_b, w_down, final_norm_g):
    B, T = x.shape[0], x.shape[1]
    lower_bounds = jnp.cumsum(jax.nn.softmax(hgrn_lb_logits.astype(jnp.float32), axis=0), axis=0)
    cond = jax.nn.silu(c)
    for l in range(DEPTH):
        mod = cond @ ada_w[l] + ada_b[l]
        shift_m, scale_m, gate_m, shift_f, scale_f, gate_f = jnp.split(mod[:, None, :], N_MOD, axis=-1)

        h = rms_norm(x, mix_norm_g[l]) * (1 + scale_m) + shift_m
        proj = h @ w_in[l] + b_in[l]
        q_a, k_a, v_a, q_h, f_h, i_h, g_h = jnp.split(proj, IN_OFFSETS, axis=-1)
        attn = sliding_window_gqa_with_sinks(
            q_a.reshape(B, T, ATTN_HEADS, ATTN_HEAD_DIM),
            k_a.reshape(B, T, ATTN_KV_HEADS, ATTN_HEAD_DIM),
            v_a.reshape(B, T, ATTN_KV_HEADS, ATTN_HEAD_DIM),
            attn_sinks[l])
        attn = rms_norm(attn, attn_out_g[l])
        rec = hgrn2_chunkwise(
            q_h.reshape(B, T, HGRN_HEADS, HGRN_KEY_DIM),
            f_h.reshape(B, T, HGRN_HEADS, HGRN_KEY_DIM),
            i_h.reshape(B, T, HGRN_HEADS, HGRN_VAL_DIM),
            lower_bounds[l].reshape(HGRN_HEADS, HGRN_KEY_DIM))
        rec = rms_norm(rec, hgrn_out_g[l].reshape(HGRN_HEADS, HGRN_VAL_DIM)) * jax.nn.silu(g_h.reshape(B, T, HGRN_HEADS, HGRN_VAL_DIM))
        mixed = jnp.concatenate([attn, rec.reshape(B, T, HGRN_WIDTH)], axis=-1) @ w_out[l]
        x = x + gate_m * mixed

        h = rms_norm(x, ffn_norm_g[l]) * (1 + scale_f) + shift_f
        u, v = jnp.split(h @ w_up[l], 2, axis=-1)
        u = causal_depthwise_conv(u, conv_w[l], conv_b[l])
        x = x + gate_f * ((jax.nn.silu(u) * v) @ w_down[l])
    return rms_norm(x, final_norm_g)
```

```python
import numpy as np
from contextlib import ExitStack
import concourse.bass as bass
import concourse.mybir as mybir
from concourse.alu_op_type import AluOpType as ALU
from concourse.bass_utils import run_bass_kernel_spmd

F32 = mybir.dt.float32
BF16 = mybir.dt.bfloat16
AF = mybir.ActivationFunctionType
AX = mybir.AxisListType

ENGS = ("pe", "act", "dve", "pool", "sp")
D = 1024
DFF = 2816
NFF = 22
EPS = 1e-6
NUNITS = 96
RS = 8
NEG = -16384.0
_INPROJ_ORDER = [0, 4, 8, 12, 1, 5, 9, 13, 2, 6, 10, 14, 3, 7, 11, 15, 16, 17, 18, 19, 20, 21]
UORDER = _INPROJ_ORDER + list(range(22, NUNITS))
UPOS = {u: i for i, u in enumerate(UORDER)}

SF_BFM = 0
SF_GMIX = 17
SF_GFFN = 25
SF_GA = 33
SF_GH = 37
SF_CW = 41
SF_CB = 107
SF_L0 = 129
SF_L1 = 133
NSF = 137
RF_FG = 0
RF_BIH = 1024
RF_BVA = 1536
RF_SINK = 1664
NROW = 1672


class Prog:
    def __init__(self, nc):
        self.nc = nc
        self.ops = []
        self.last_w = {}
        self.readers = {}
        self.dma_readers = {}
        self.dma_last = {}
        self.eng_last = {e: None for e in ENGS}
        self.epoch = 0

    def op(self, eng, name, *args, reads=(), writes=(), dma=None, extra_deps=(), **kw):
        fn = None if name is None else (name, args, kw)
        idx = len(self.ops)
        deps = set(extra_deps)
        for r in reads:
            t = self.last_w.get(r)
            if t is not None:
                deps.add(t)
        for w in writes:
            t = self.last_w.get(w)
            if t is not None:
                deps.add(t)
            deps.update(self.readers.get(w, {}).values())
            deps.update(self.dma_readers.get(w, ()))
        if dma is not None:
            t = self.dma_last.get(dma)
            if t is not None:
                deps.add(t)
            self.dma_last[dma] = idx
        deps.discard(idx)
        self.ops.append(dict(eng=eng, fn=fn, deps=deps, dma=dma, epoch=self.epoch))
        for r in reads:
            if dma is not None:
                self.dma_readers.setdefault(r, []).append(idx)
            else:
                self.readers.setdefault(r, {})[eng] = idx
        for w in writes:
            self.last_w[w] = idx
            self.readers[w] = {}
            self.dma_readers[w] = []
        self.eng_last[eng] = idx
        return idx

    def barrier(self, engs=ENGS, skip_dma=()):
        deps = set(i for i in self.eng_last.values() if i is not None)
        deps |= set(v for k, v in self.dma_last.items() if not str(k).startswith(skip_dma or "\0"))
        for e in engs:
            self.op(e, None, extra_deps=deps)

    def emit(self, get_sem, dma_sems):
        nc = self.nc
        engobj = dict(pe=nc.tensor, act=nc.scalar, dve=nc.vector, pool=nc.gpsimd, sp=nc.sync)
        ops = self.ops

        def resolve(o):
            out = []
            stack = list(o["deps"])
            visited = set()
            while stack:
                d = stack.pop()
                if d in visited:
                    continue
                visited.add(d)
                od = ops[d]
                if od["fn"] is None:
                    stack.extend(od["deps"])
                    continue
                if od["dma"] is None and od["eng"] == "pe" and o["eng"] == "pe" and o["dma"] is None:
                    continue
                out.append(d)
            return out

        needed = set()
        for o in ops:
            o["rdeps"] = resolve(o)
            for d in o["rdeps"]:
                if ops[d]["dma"] is None:
                    needed.add(d)
        cnt = {}
        dcnt = {}
        for i, o in enumerate(ops):
            if o["fn"] is None:
                continue
            if o["dma"] is not None:
                dcnt[o["dma"]] = dcnt.get(o["dma"], 0) + 16
                o["tok"] = (("dma", o["dma"]), dcnt[o["dma"]])
            elif i in needed:
                k = ("eng", o["eng"], o["epoch"])
                cnt[k] = cnt.get(k, 0) + 1
                o["tok"] = (k, cnt[k])
        self.maxcnt = (max(cnt.values()) if cnt else 0, max(dcnt.values()) if dcnt else 0)
        seen = {e: {} for e in ENGS}
        nwaits = 0
        for i, o in enumerate(ops):
            E = o["eng"]
            eo = engobj[E]
            waits = {}
            for d in o["rdeps"]:
                key, val = ops[d]["tok"]
                if waits.get(key, 0) < val:
                    waits[key] = val
            for key, val in waits.items():
                if seen[E].get(key, 0) < val:
                    sem = dma_sems[key[1]] if key[0] == "dma" else get_sem(key[1], key[2])
                    eo.wait_ge(sem, val)
                    seen[E][key] = val
                    nwaits += 1
            if o["fn"] is None:
                continue
            nm, ar, kw = o["fn"]
            ins = getattr(eo, nm)(*ar, **kw)
            if o["dma"] is not None:
                ins.then_inc(dma_sems[o["dma"]], 16)
            elif i in needed:
                ins.then_inc(get_sem(E, o["epoch"]), 1)
        return nwaits


class Arena:
    def __init__(self, ap):
        self.ap = ap
        self.off = 0

    def take(self, shape, dt):
        n = int(np.prod(shape))
        esz = 4 if dt == F32 else 2
        nw = (n * esz + 3) // 4
        sl = self.ap[:, self.off:self.off + nw]
        self.off += nw
        assert self.off <= self.ap.shape[1], ("arena overflow", self.off, self.ap.shape)
        v = sl if dt == F32 else sl.bitcast(dt)
        if len(shape) == 1:
            return v
        if len(shape) == 2:
            return v.rearrange("p (a b) -> p a b", a=shape[0], b=shape[1])
        if len(shape) == 3:
            return v.rearrange("p (a b c) -> p a b c", a=shape[0], b=shape[1], c=shape[2])
        raise ValueError(shape)


def build_nc(T, NSEQ, dbg=False):
    NT = T // 512
    NTT = NSEQ * NT
    nc = bass.Bass("TRN2", target_bir_lowering=False)
    xin = nc.dram_tensor("xin", [NSEQ * T, D], F32, kind="ExternalInput").ap()
    wun = nc.dram_tensor("wun", [NUNITS * 128, 1024], F32, kind="ExternalInput").ap()
    adaun = nc.dram_tensor("adaun", [12, 128, 8, 512], F32, kind="ExternalInput").ap()
    adab = nc.dram_tensor("adab", [6144], F32, kind="ExternalInput").ap()
    cT = nc.dram_tensor("cT", [128, 8 * NSEQ], F32, kind="ExternalInput").ap()
    smallF_d = nc.dram_tensor("smallF", [128, NSF], F32, kind="ExternalInput").ap()
    rowF_d = nc.dram_tensor("rowF", [NROW], F32, kind="ExternalInput").ap()
    yout_d = nc.dram_tensor("y", [NSEQ * T, D], F32, kind="ExternalOutput").ap()
    wbf = nc.dram_tensor("wbf", [NUNITS * 128, 1024], BF16, kind="Internal").ap()

    with ExitStack() as es:
        def sb(name, shape, dt):
            return es.enter_context(nc.sbuf_tensor(name, shape, dt))[:]

        NEP = (NTT + 1) // 2 + 1
        sems = {(e, ep): es.enter_context(nc.semaphore("s_%s%d" % (e, ep)))
                for e in ("pe", "act", "dve", "pool") for ep in range(NEP)}
        NRG = (NTT + 3) // 4
        dma_keys = (["xl0", "xl1", "yo0", "yo1", "c0", "c1", "c2", "c2b", "c3", "ad0", "ad1"]
                    + ["rg%d_%d" % (i, q) for i in range(RS) for q in range(NRG)] + ["cast%d" % i for i in range(12)])
        dsem = {k: es.enter_context(nc.semaphore("d_" + k)) for k in dma_keys}
        P = Prog(nc)

        xbuf = sb("xbuf", [128, 8, D], F32)
        xn = sb("xn", [128, 4, D], BF16)
        hT = sb("hT", [128, 8, 512], BF16)
        yb = sb("yb", [128, 2, D], F32)
        ring = sb("ring", [128, RS, 1024], BF16)
        KaT = sb("KaT", [128, 2, 2, 512], BF16)
        Vaug = sb("Vaug", [128, 2, 4, 2, 65], BF16)
        Sf = sb("Sf", [128, 4, 128], F32)
        Sbf = sb("Sbf", [128, 2, 4, 128], BF16)
        uprev = sb("uprev", [128, NFF, 2], F32)
        Qe = sb("Qe", [128, 4, 512], BF16)
        Qo = sb("Qo", [128, 4, 512], BF16)
        Khe = sb("Khe", [128, 4, 4, 128], BF16)
        Kho = sb("Kho", [128, 4, 4, 128], BF16)
        ident = sb("ident", [128, 128], BF16)
        identf = sb("identf", [128, 128], F32)
        maskcur = sb("maskcur", [128, 4, 128], BF16)
        maskprev = sb("maskprev", [128, 4, 128], BF16)
        hmask = sb("hmask", [128, 4, 128], BF16)
        scanmask = sb("scanmask", [128, 512], F32)
        neghalf = sb("neghalf", [128, 8], F32)
        smallF = sb("smallFs", [128, NSF], F32)
        rowbc = sb("rowbc", [128, NROW], F32)
        gates = sb("gates", [128, NSEQ, 2, D], F32)
        modT = sb("modT", [128, 4, NSEQ, 8], F32)
        am = sb("am", [128, NSEQ, 8], F32)
        af = sb("af", [128, NSEQ, 8], F32)
        hc = sb("hc", [128, 8, 4], F32)
        expsink = sb("expsink", [128, 8], F32)
        el = sb("el", [128, 4, 8], F32)
        stat = sb("stat", [128, 2, 32], F32)
        arena_t = sb("arena", [128, 19968], F32)

        psb = [es.enter_context(nc.psum_tensor("ps%d" % i, [128, 512], F32))[:] for i in range(8)]
        bank_ctr = [0]

        def bank():
            k = bank_ctr[0] % 8
            bank_ctr[0] += 1
            return k

        MA = Arena(arena_t)
        PT = MA.take([2, 2, 512], BF16)
        attn = MA.take([2, 512], F32)
        attn_n = MA.take([4, 512], BF16)
        rec_n = MA.take([4, 512], BF16)
        Vh = MA.take([4, 512], BF16)
        ST = MA.take([2, 4, 128], BF16)
        osq = MA.take([512], F32)
        osq2 = MA.take([512], F32)
        mixT = MA.take([8, 512], BF16)
        otmp = MA.take([2, 512], F32)
        late_top = MA.off
        QaT = MA.take([4, 512], BF16)
        tA2 = MA.take([2, 512], F32)
        tB2 = MA.take([2, 512], F32)
        tC2 = MA.take([2, 512], F32)
        tD2 = MA.take([2, 512], F32)
        tE2 = MA.take([2, 512], F32)
        silug = MA.take([4, 512], F32)
        KtT = MA.take([4, 512], BF16)
        KhatT = MA.take([4, 512], BF16)
        mixer_top = MA.off
        FA = Arena(arena_t)
        gT = FA.take([NFF, 512], BF16)
        ubuf = FA.take([2, 516], F32)
        sgb = FA.take([2, 512], F32)
        dtmp = FA.take([2, 512], F32)
        fjunk = FA.take([1024], BF16)
        assert FA.off <= late_top, (FA.off, late_top)
        PA = Arena(arena_t)
        adab_bc = PA.take([2, 512], F32)
        adaf = PA.take([2, 8, 512], F32)
        adaw = PA.take([2, 8, 512], BF16)
        mtmp = PA.take([2, 512], F32)
        mtmp2 = PA.take([2, 512], F32)
        maskf = PA.take([4, 128], F32)
        cTs = PA.take([8 * NSEQ], F32)
        cond = PA.take([8 * NSEQ], F32)
        condrep = PA.take([8 * NSEQ, 128], BF16)

        A = P.op

        A("pool", "dma_start", out=smallF, in_=smallF_d, writes=["smallF"], dma="c0")
        A("pool", "dma_start", out=rowbc, in_=rowF_d.partition_broadcast(128), writes=["rowbc"], dma="c1")
        A("pool", "dma_start", out=cTs, in_=cT, writes=["cTs"], dma="c2")
        def cast(k):
            A("pool", "dma_start", out=wbf[k * 1024:(k + 1) * 1024, :], in_=wun[k * 1024:(k + 1) * 1024, :],
              reads=(["adaf0", "adaf1"] if k >= 2 else []), writes=["wbf%d" % k], dma="cast%d" % k)
        def xload(g):
            base = (g % 2) * 4
            r0 = g * 512
            A("pool", "dma_start", out=xbuf[:, base:base + 4, :],
                                            in_=xin[r0:r0 + 512, :].rearrange("(s p) d -> p s d", p=128),
              writes=["x%d" % (base + s) for s in range(4)], dma="xl%d" % (g % 2))
        xload(0)
        cast(0)
        cast(1)

        A("pool", "memset", identf, 0.0, writes=["identf"])
        A("pool", "affine_select", out=identf, in_=identf, pattern=[[-1, 128]], compare_op=ALU.not_equal,
                                            fill=1.0, base=0, channel_multiplier=1, reads=["identf"], writes=["identf"])
        A("dve", "tensor_copy", out=ident, in_=identf, reads=["identf"], writes=["ident"])
        A("pool", "memset", maskf, 0.0, writes=["maskf"])
        A("pool", "affine_select", out=maskf, in_=maskf, pattern=[[0, 4], [1, 128]], compare_op=ALU.is_ge,
                                            fill=NEG, base=0, channel_multiplier=-1, reads=["maskf"], writes=["maskf"])
        A("dve", "tensor_copy", out=maskcur, in_=maskf, reads=["maskf"], writes=["maskcur"])
        A("pool", "memset", maskf, 0.0, reads=["maskf"], writes=["maskf"])
        A("pool", "affine_select", out=maskf, in_=maskf, pattern=[[0, 4], [-1, 128]], compare_op=ALU.is_gt,
                                            fill=NEG, base=0, channel_multiplier=1, reads=["maskf"], writes=["maskf"])
        A("dve", "tensor_copy", out=maskprev, in_=maskf, reads=["maskf"], writes=["maskprev"])
        A("pool", "memset", maskf, 1.0, reads=["maskf"], writes=["maskf"])
        A("pool", "affine_select", out=maskf, in_=maskf, pattern=[[0, 4], [1, 128]], compare_op=ALU.is_ge,
                                            fill=0.0, base=0, channel_multiplier=-1, reads=["maskf"], writes=["maskf"])
        A("pool", "memset", maskf[0:64, :, 64:128], 0.0, reads=["maskf"], writes=["maskf"])
        A("dve", "tensor_copy", out=hmask, in_=maskf, reads=["maskf"], writes=["hmask"])
        A("pool", "memset", scanmask, 0.0, writes=["scanmask"])
        A("pool", "memset", scanmask.rearrange("p (c t) -> p c t", t=64)[:, :, 0:1], 1.0,
          reads=["scanmask"], writes=["scanmask"])
        A("pool", "memset", neghalf, -0.5, writes=["neghalf"])
        A("pool", "memset", KaT, 0.0, writes=["KaT0_0", "KaT0_1", "KaT1_0", "KaT1_1"])
        A("pool", "memset", Vaug, 1.0, writes=["Vaug0", "Vaug1"])
        A("pool", "memset", Qe, 0.0, writes=["Qe%d" % c for c in range(4)])
        A("pool", "memset", Qo, 0.0, writes=["Qo%d" % c for c in range(4)])
        A("pool", "memset", Khe, 0.0, writes=["Khe%d" % c for c in range(4)])
        A("pool", "memset", Kho, 0.0, writes=["Kho%d" % c for c in range(4)])

        lb, oml, homl, nhoml, lbh, halfbf = (hc[:, i, :] for i in range(6))
        A("dve", "tensor_tensor", out=hc[:, 6, :], in0=smallF[:, SF_L0:SF_L0 + 4], in1=smallF[:, SF_L1:SF_L1 + 4],
                                           op=ALU.subtract, reads=["smallF"], writes=["hc6"])
        A("act", "activation", out=lb, in_=hc[:, 6, :], func=AF.Sigmoid, reads=["hc6"], writes=["lb"])
        A("dve", "tensor_scalar", out=oml, in0=lb, scalar1=-1.0, scalar2=1.0, op0=ALU.mult, op1=ALU.add,
          reads=["lb"], writes=["oml"])
        A("dve", "tensor_scalar", out=homl, in0=oml, scalar1=0.5, scalar2=None, op0=ALU.mult,
          reads=["oml"], writes=["homl"])
        A("dve", "tensor_scalar", out=nhoml, in0=oml, scalar1=-0.5, scalar2=None, op0=ALU.mult,
          reads=["oml"], writes=["nhoml"])
        A("dve", "tensor_tensor", out=lbh, in0=lb, in1=homl, op=ALU.add, reads=["lb", "homl"], writes=["lbh"])
        A("dve", "tensor_scalar", out=halfbf, in0=smallF[:, SF_BFM:SF_BFM + 4], scalar1=0.5, scalar2=None,
                                           op0=ALU.mult, reads=["smallF"], writes=["halfbf"])
        HC = ["lb", "oml", "homl", "nhoml", "lbh", "halfbf"]
        A("act", "activation", out=expsink, in_=rowbc[:, RF_SINK:RF_SINK + 8], func=AF.Exp,
          reads=["rowbc"], writes=["expsink"])

        A("act", "activation", out=cond, in_=cTs, func=AF.Silu, reads=["cTs"], writes=["cond"])
        A("dve", "tensor_copy", out=condrep, in_=cond.unsqueeze(2).to_broadcast([128, 8 * NSEQ, 128]),
          reads=["cond"], writes=["condrep"])
        FIELD = {0: 0, 1: 1, 3: 2, 4: 3}
        for cb in range(12):
            ab = cb % 2
            A("sp", "dma_start", out=adaf[:, ab, :, :], in_=adaun[cb], writes=["adaf%d" % ab], dma="ad%d" % ab)
            A("pool", "dma_start", out=adab_bc[:, ab, :], in_=adab[cb * 512:(cb + 1) * 512].partition_broadcast(128),
              writes=["adab%d" % ab], dma="c3" if ab else "c2b")
            A("act", "activation", out=adaw[:, ab, 0:4, :], in_=adaf[:, ab, 0:4, :], func=AF.Identity,
              reads=["adaf%d" % ab], writes=["adaw%d" % ab])
            A("dve", "tensor_copy", out=adaw[:, ab, 4:8, :], in_=adaf[:, ab, 4:8, :],
              reads=["adaf%d" % ab], writes=["adaw%d_b" % ab])
            field, half = cb // 2, cb % 2
            for b in range(NSEQ):
                bk = bank()
                for kc in range(8):
                    A("pe", "matmul",
                        psb[bk], lhsT=condrep[:, kc * NSEQ + b, :], rhs=adaw[:, ab, kc, :],
                        start=(kc == 0), stop=(kc == 7),
                      reads=["condrep", "adaw%d" % ab, "adaw%d_b" % ab], writes=["ps%d" % bk])
                if field in (2, 5):
                    gi = 0 if field == 2 else 1
                    A("dve", "tensor_tensor",
                        out=gates[:, b, gi, half * 512:(half + 1) * 512], in0=psb[bk], in1=adab_bc[:, ab, :], op=ALU.add,
                      reads=["ps%d" % bk, "adab%d" % ab], writes=["gates"])
                else:
                    fi = FIELD[field]
                    mb = b
                    A("dve", "tensor_tensor",
                        out=mtmp[:, mb, :], in0=psb[bk], in1=adab_bc[:, ab, :], op=ALU.add,
                      reads=["ps%d" % bk, "adab%d" % ab], writes=["mtmp%d" % mb])
                    A("dve", "tensor_tensor",
                        out=mtmp2[:, mb, :].rearrange("p (a b) -> p a b", a=4),
                        in0=mtmp[:, mb, :].rearrange("p (a b) -> p a b", a=4),
                        in1=identf.unsqueeze(1).to_broadcast([128, 4, 128]), op=ALU.mult,
                      reads=["mtmp%d" % mb, "identf"], writes=["mtmp2%d" % mb])
                    A("dve", "tensor_reduce",
                        out=modT[:, fi, b, half * 4:(half + 1) * 4],
                        in_=mtmp2[:, mb, :].rearrange("p (a b) -> p a b", a=4), axis=AX.X, op=ALU.add,
                      reads=["mtmp2%d" % mb], writes=["modT"])
        cast(2)
        cast(3)
        for b in range(NSEQ):
            A("dve", "scalar_tensor_tensor", out=am[:, b, :], in0=modT[:, 1, b, :], scalar=1.0,
                                                           in1=smallF[:, SF_GMIX:SF_GMIX + 8], op0=ALU.add, op1=ALU.mult,
              reads=["modT", "smallF"], writes=["am"])
            A("dve", "scalar_tensor_tensor", out=af[:, b, :], in0=modT[:, 3, b, :], scalar=1.0,
                                                           in1=smallF[:, SF_GFFN:SF_GFFN + 8], op0=ALU.add, op1=ALU.mult,
              reads=["modT", "smallF"], writes=["af"])
        P.barrier(skip_dma=("cast",))

        unit_ctr = [0]
        cur_g = [0]

        def load_unit(uidx):
            slot = unit_ctr[0] % RS
            unit_ctr[0] += 1
            upos = UPOS[uidx]
            A("sp", "dma_start", out=ring[:, slot, :], in_=wbf[upos * 128:(upos + 1) * 128, :],
              reads=["wbf%d" % (upos // 8)], writes=["rg%d" % slot], dma="rg%d_%d" % (slot, cur_g[0] // 4))
            return slot

        def norm_stats(g, tagp):
            par = g % 2
            ss = stat[:, par, 0:4] if tagp == 0 else stat[:, par, 8:12]
            rs = stat[:, par, 4:8] if tagp == 0 else stat[:, par, 12:16]
            sn = "n%d_%d" % (tagp, par)
            for s in range(4):
                sl = (g % 2) * 4 + s
                A("act", "activation", out=xn[:, s, :], in_=xbuf[:, sl, :], func=AF.Square,
                  accum_out=ss[:, s:s + 1], reads=["x%d" % sl], writes=["xn%d" % s, sn + "ss%d" % s])
            A("pool", "tensor_scalar", out=ss, in0=ss, scalar1=1.0 / D, scalar2=EPS, op0=ALU.mult, op1=ALU.add,
              reads=[sn + "ss%d" % s for s in range(4)], writes=[sn + "ssb"])
            A("pool", "tensor_tensor", out=rs, in0=ss, in1=neghalf[:, 0:4], op=ALU.pow,
              reads=[sn + "ssb", "neghalf"] + [sn + "ss%d" % s for s in range(4)], writes=[sn + "rs"])
            for s in range(4):
                sl = (g % 2) * 4 + s
                if s % 2 == 0:
                    A("dve", "tensor_scalar", out=xn[:, s, :], in0=xbuf[:, sl, :], scalar1=rs[:, s:s + 1],
                      scalar2=None, op0=ALU.mult, reads=["x%d" % sl, sn + "rs"], writes=["xn%d" % s])
                else:
                    A("act", "activation", out=xn[:, s, :], in_=xbuf[:, sl, :], func=AF.Identity, scale=rs[:, s:s + 1],
                      reads=["x%d" % sl, sn + "rs"], writes=["xn%d" % s])

        def norm_trans(b, a_t, sh_ap):
            for c in range(8):
                bk = bank()
                pb = psb[bk].bitcast(BF16)
                for s in range(4):
                    A("pe", "transpose", out=pb[:, s * 128:(s + 1) * 128], in_=xn[:, s, c * 128:(c + 1) * 128],
                      identity=ident, reads=["xn%d" % s, "ident"], writes=["ps%d" % bk])
                if c % 2 == 0:
                    A("act", "activation", out=hT[:, c, :], in_=pb[:, 0:512], func=AF.Identity,
                      scale=a_t[:, b, c:c + 1], bias=sh_ap[:, b, c:c + 1],
                      reads=["ps%d" % bk, "am", "af", "modT"], writes=["hT%d" % c])
                else:
                    A("dve", "tensor_scalar", out=hT[:, c, :], in0=pb[:, 0:512], scalar1=a_t[:, b, c:c + 1],
                      scalar2=sh_ap[:, b, c:c + 1], op0=ALU.mult, op1=ALU.add,
                      reads=["ps%d" % bk, "am", "af", "modT"], writes=["hT%d" % c])

        HTALL = ["hT%d" % c for c in range(8)]

        def fm_chunk(uidx):
            slot = load_unit(uidx)
            bk = bank()
            for kc in range(8):
                A("pe", "matmul", psb[bk], lhsT=ring[:, slot, kc * 128:(kc + 1) * 128], rhs=hT[:, kc, :],
                  start=(kc == 0), stop=(kc == 7), reads=["rg%d" % slot] + HTALL, writes=["ps%d" % bk])
            return bk

        def bias(f):
            return smallF[:, SF_BFM + f:SF_BFM + f + 1]

        ARENA_ENGS = ("act", "dve", "pool")

        norm_stats(0, 0)
        norm_trans(0, am, modT[:, 0, :, :])
        for g in range(NTT):
            b = g // NT
            ti = g % NT
            par = g % 2
            cur_g[0] = g
            P.epoch = 1 + g // 2
            if g + 1 < NTT:
                xload(g + 1)
            if ti == 0:
                A("pool", "memset", Sf, 0.0, writes=["Sf0", "Sf1", "Sf2", "Sf3"])
                A("pool", "memset", Sbf[:, 0, :, :], 0.0, writes=["Sbf0"])
                A("pool", "memset", uprev, 0.0, writes=["uprev%d" % j for j in range(NFF)])

            def hg_pre(c):
                hp = c % 2
                tA, tB, tC, tD, tE = tA2[:, hp, :], tB2[:, hp, :], tC2[:, hp, :], tD2[:, hp, :], tE2[:, hp, :]
                bk = fm_chunk(c)
                A("act", "activation", out=tA, in_=psb[bk], func=AF.Tanh, scale=0.5, bias=halfbf[:, c:c + 1],
                  reads=["ps%d" % bk] + HC, writes=["tA%d" % hp])
                bk2 = fm_chunk(4 + c)
                A("act", "activation", out=tE, in_=psb[bk2], func=AF.Silu, bias=bias(4 + c),
                  reads=["ps%d" % bk2, "smallF"], writes=["tE%d" % hp])
                A("act", "activation", out=tB, in_=tA, func=AF.Identity, scale=nhoml[:, c:c + 1], bias=homl[:, c:c + 1],
                  reads=["tA%d" % hp] + HC, writes=["tB%d" % hp])
                A("act", "activation", out=tC, in_=tA, func=AF.Identity, scale=homl[:, c:c + 1], bias=lbh[:, c:c + 1],
                  reads=["tA%d" % hp] + HC, writes=["tC%d" % hp])
                A("dve", "tensor_tensor_scan", out=tD, data0=scanmask, data1=tC, initial=0.0, op0=ALU.max,
                  op1=ALU.mult, reads=["tC%d" % hp, "scanmask"], writes=["tD%d" % hp])
                A("dve", "reciprocal", out=tC, in_=tD, reads=["tD%d" % hp], writes=["tC%d" % hp])
                tE4 = tE.rearrange("p (j two t) -> p j two t", two=2, t=64)
                tD4 = tD.rearrange("p (j two t) -> p j two t", two=2, t=64)
                Qe4 = Qe[:, c, :].rearrange("p (j two t) -> p j two t", two=2, t=64)
                Qo4 = Qo[:, c, :].rearrange("p (j two t) -> p j two t", two=2, t=64)
                A("dve", "tensor_tensor", out=tB, in0=tB, in1=tC, op=ALU.mult, reads=["tB%d" % hp, "tC%d" % hp], writes=["tB%d" % hp])
                tB3 = tB.rearrange("p (j t) -> p j t", t=64)
                tD3 = tD.rearrange("p (j t) -> p j t", t=64)
                A("dve", "tensor_tensor", out=KhatT[:, c, :].rearrange("p (j t) -> p j t", t=64), in0=tB3,
                  in1=tD3[:, :, 63:64].to_broadcast([128, 8, 64]), op=ALU.mult,
                  reads=["tB%d" % hp, "tD%d" % hp], writes=["KhatT%d" % c])
                A("dve", "tensor_tensor", out=Qe4[:, :, 0, :], in0=tE4[:, :, 0, :], in1=tD4[:, :, 0, :], op=ALU.mult,
                  reads=["tE%d" % hp, "tD%d" % hp], writes=["Qe%d" % c])
                A("dve", "tensor_tensor", out=Qo4[:, :, 1, :], in0=tE4[:, :, 1, :], in1=tD4[:, :, 1, :], op=ALU.mult,
                  reads=["tE%d" % hp, "tD%d" % hp], writes=["Qo%d" % c])
                A("pool", "tensor_copy", out=KtT[:, c, :], in_=tB, reads=["tB%d" % hp], writes=["KtT%d" % c])
                A("pool", "tensor_copy", out=el[:, c, :], in_=tD3[:, :, 63], reads=["tD%d" % hp], writes=["el%d" % c])

            def hg_trans(c):
                bk3 = bank()
                pb = psb[bk3].bitcast(BF16)
                for s in range(4):
                    A("pe", "transpose", out=pb[:, s * 128:(s + 1) * 128], in_=KhatT[:, c, s * 128:(s + 1) * 128],
                      identity=ident, reads=["KhatT%d" % c, "ident"], writes=["ps%d" % bk3])
                pb3 = pb[:, 0:512].rearrange("p (s k) -> p s k", s=4)
                A("act", "activation", out=Khe[0:64, :, c, :], in_=pb3[0:64], func=AF.Identity,
                  reads=["ps%d" % bk3], writes=["Khe%d" % c])
                A("act", "activation", out=Kho[64:128, :, c, :], in_=pb3[64:128], func=AF.Identity,
                  reads=["ps%d" % bk3], writes=["Kho%d" % c])

            def g_chunk(c):
                bk = fm_chunk(8 + c)
                A("act", "activation", out=silug[:, c, :], in_=psb[bk], func=AF.Silu, bias=bias(8 + c),
                  reads=["ps%d" % bk, "smallF"], writes=["silug%d" % c])

            def ka_chunk():
                bk = fm_chunk(12)
                A("act", "activation", out=KaT[0:64, par, 0, :], in_=psb[bk][0:64], func=AF.Identity,
                  bias=bias(12)[0:64], reads=["ps%d" % bk, "smallF"], writes=["KaT%d_0" % par])
                A("act", "activation", out=KaT[64:128, par, 1, :], in_=psb[bk][64:128], func=AF.Identity,
                  bias=bias(12)[64:128], reads=["ps%d" % bk, "smallF"], writes=["KaT%d_1" % par])

            def qa_chunk(gq):
                bk = fm_chunk(13 + gq)
                A("act", "activation", out=QaT[:, gq, :], in_=psb[bk], func=AF.Identity, bias=bias(13 + gq),
                  reads=["ps%d" % bk, "smallF"], writes=["QaT%d" % gq])

            def ih_group():
                bks = [bank() for _ in range(4)]
                for u in range(4):
                    slot = load_unit(17 + u)
                    for kk in range(2):
                        kc = 2 * u + kk
                        for s in range(4):
                            A("pe", "matmul", psb[bks[s]], lhsT=hT[:, kc, s * 128:(s + 1) * 128],
                              rhs=ring[:, slot, kk * 512:(kk + 1) * 512], start=(kc == 0), stop=(kc == 7),
                              reads=["rg%d" % slot, "hT%d" % kc], writes=["ps%d" % bks[s]])
                for s in range(4):
                    A("dve", "tensor_tensor", out=Vh[:, s, :], in0=psb[bks[s]], in1=rowbc[:, RF_BIH:RF_BIH + 512],
                      op=ALU.add, reads=["ps%d" % bks[s], "rowbc"], writes=["Vh%d" % s])

            def va_group():
                slot = load_unit(21)
                bk = bank()
                for s in range(4):
                    for kc in range(8):
                        A("pe", "matmul", psb[bk][:, s * 128:(s + 1) * 128], lhsT=hT[:, kc, s * 128:(s + 1) * 128],
                          rhs=ring[:, slot, kc * 128:(kc + 1) * 128], start=(kc == 0), stop=(kc == 7),
                          reads=["rg%d" % slot, "hT%d" % kc], writes=["ps%d" % bk])
                A("dve", "tensor_tensor", out=Vaug[:, par, :, :, 0:64],
                  in0=psb[bk].rearrange("p (s h d) -> p s h d", s=4, h=2),
                  in1=rowbc[:, RF_BVA:RF_BVA + 128].rearrange("p (h d) -> p h d", h=2).unsqueeze(1).to_broadcast([128, 4, 2, 64]),
                  op=ALU.add, reads=["ps%d" % bk, "rowbc"], writes=["Vaug%d" % par])

            hg_pre(0); g_chunk(0); ka_chunk()
            if g == 0:
                cast(4)
            hg_pre(1); g_chunk(1); qa_chunk(0)
            hg_pre(2); g_chunk(2); qa_chunk(1); hg_trans(0)
            if g == 0:
                cast(5)
            hg_pre(3); g_chunk(3); qa_chunk(2); hg_trans(1)
            qa_chunk(3)
            if g > 0:
                P.barrier(ARENA_ENGS, skip_dma=("cast",))
            ih_group(); hg_trans(2); va_group(); hg_trans(3)
            if g == 0:
                cast(6)
                cast(7)

            KTT = ["KtT%d" % c for c in range(4)]
            QE = ["Qe%d" % c for c in range(4)]
            QO = ["Qo%d" % c for c in range(4)]
            KHE = ["Khe%d" % c for c in range(4)]
            KHO = ["Kho%d" % c for c in range(4)]
            EL = ["el%d" % c for c in range(4)]
            QAT = ["QaT%d" % c for c in range(4)]

            pst = stat[:, par, :]
            pending_tail = []
            for s in range(4):
                first = (ti == 0 and s == 0)
                cols = slice(s * 128, (s + 1) * 128)
                sb_ = s % 2
                bsc = bank()
                for c in range(4):
                    A("pe", "matmul", psb[bsc][:, c * 128:(c + 1) * 128], lhsT=KtT[:, c, cols], rhs=Qe[:, c, cols],
                      start=True, stop=False, reads=["KtT%d" % c, "Qe%d" % c], writes=["ps%d" % bsc])
                    A("pe", "matmul", psb[bsc][:, c * 128:(c + 1) * 128], lhsT=KtT[:, c, cols], rhs=Qo[:, c, cols],
                      start=False, stop=True, reads=["KtT%d" % c, "Qo%d" % c], writes=["ps%d" % bsc])

                def kv_mm(khat, khnames):
                    bkv = bank()
                    for c in range(4):
                        A("pe", "matmul", psb[bkv][:, c * 128:(c + 1) * 128], lhsT=khat[:, s, c, :],
                          rhs=Vh[:, s, c * 128:(c + 1) * 128], start=True, stop=True,
                          reads=[khnames[c], "Vh%d" % s], writes=["ps%d" % bkv])
                    return bkv

                def state_upd(bkv, j, dst):
                    for c in range(4):
                        A("dve", "scalar_tensor_tensor", out=Sf[:, c, :], in0=Sf[:, c, :], scalar=el[:, c, j:j + 1],
                          in1=psb[bkv][:, c * 128:(c + 1) * 128], op0=ALU.mult, op1=ALU.add,
                          reads=["Sf%d" % c, "el%d" % c, "ps%d" % bkv], writes=["Sf%d" % c])
                    A("act", "activation", out=Sbf[:, dst, :, :], in_=Sf, func=AF.Identity,
                      reads=["Sf0", "Sf1", "Sf2", "Sf3"], writes=["Sbf%d" % dst])

                bkv_e = kv_mm(Khe, KHE)
                prevs = {}
                for h in range(2):
                    bc_ = bank()
                    A("pe", "matmul", psb[bc_], lhsT=KaT[:, par, h, cols], rhs=QaT[:, :, cols], start=True, stop=False,
                      reads=["KaT%d_%d" % (par, h)] + QAT, writes=["ps%d" % bc_])
                    A("pe", "matmul", psb[bc_], lhsT=ident, rhs=maskcur.rearrange("p a b -> p (a b)"),
                      start=False, stop=True, reads=["ident", "maskcur"], writes=["ps%d" % bc_])
                    A("act", "activation", out=PT[:, h, 1, :], in_=psb[bc_], func=AF.Exp, scale=0.125,
                      reads=["ps%d" % bc_], writes=["PTc%d" % h])
                    if not first:
                        bp_ = bank()
                        if s > 0:
                            kprev = KaT[:, par, h, (s - 1) * 128:s * 128]
                            kname = "KaT%d_%d" % (par, h)
                            vprev = Vaug[:, par, s - 1, h, :]
                            vname = "Vaug%d" % par
                        else:
                            kprev = KaT[:, 1 - par, h, 384:512]
                            kname = "KaT%d_%d" % (1 - par, h)
                            vprev = Vaug[:, 1 - par, 3, h, :]
                            vname = "Vaug%d" % (1 - par)
                        prevs[h] = (vprev, vname)
                        A("pe", "matmul", psb[bp_], lhsT=kprev, rhs=QaT[:, :, cols], start=True, stop=False,
                          reads=[kname] + QAT, writes=["ps%d" % bp_])
                        A("pe", "matmul", psb[bp_], lhsT=ident, rhs=maskprev.rearrange("p a b -> p (a b)"),
                          start=False, stop=True, reads=["ident", "maskprev"], writes=["ps%d" % bp_])
                        A("act", "activation", out=PT[:, h, 0, :], in_=psb[bp_], func=AF.Exp, scale=0.125,
                          reads=["ps%d" % bp_], writes=["PTp%d" % h])
                A("dve", "tensor_tensor", out=ST[:, sb_, :, :], in0=psb[bsc].rearrange("p (c t) -> p c t", c=4),
                  in1=hmask, op=ALU.mult, reads=["ps%d" % bsc, "hmask"], writes=["ST%d" % sb_])
                state_upd(bkv_e, 2 * s, 1)
                while pending_tail:
                    pending_tail.pop(0)()
                if s > 0:
                    bank()
                for h in range(2):
                    bo_ = bank()
                    for gq in range(4):
                        oreg = psb[bo_][:, gq * 65:(gq + 1) * 65]
                        A("pe", "matmul", oreg, lhsT=PT[:, h, 1, gq * 128:(gq + 1) * 128], rhs=Vaug[:, par, s, h, :],
                          start=True, stop=first, reads=["PTc%d" % h, "Vaug%d" % par], writes=["ps%d" % bo_])
                        if not first:
                            A("pe", "matmul", oreg, lhsT=PT[:, h, 0, gq * 128:(gq + 1) * 128], rhs=prevs[h][0],
                              start=False, stop=True, reads=["PTp%d" % h, prevs[h][1]], writes=["ps%d" % bo_])
                    o3 = psb[bo_][:, 0:260].rearrange("p (g d) -> p g d", g=4)
                    den = pst[:, 16 + h * 4:20 + h * 4]
                    dn = "den%d%d" % (par, h)
                    A("dve", "tensor_tensor", out=den, in0=o3[:, :, 64], in1=expsink[:, h * 4:(h + 1) * 4], op=ALU.add,
                      reads=["ps%d" % bo_, "expsink"], writes=[dn])
                    A("dve", "reciprocal", out=den, in_=den, reads=[dn], writes=[dn])
                    A("dve", "tensor_tensor", out=attn[:, sb_, h * 256:(h + 1) * 256].rearrange("p (g d) -> p g d", g=4),
                      in0=o3[:, :, 0:64], in1=den.unsqueeze(2).to_broadcast([128, 4, 64]), op=ALU.mult,
                      reads=["ps%d" % bo_, dn], writes=["attn%d_%d" % (sb_, h)])
                bo2 = bank()
                for c in range(4):
                    oreg = psb[bo2][:, c * 128:(c + 1) * 128]
                    A("pe", "matmul", oreg, lhsT=ST[:, sb_, c, :], rhs=Vh[:, s, c * 128:(c + 1) * 128],
                      start=True, stop=False, reads=["ST%d" % sb_, "Vh%d" % s], writes=["ps%d" % bo2])
                    A("pe", "matmul", oreg, lhsT=Qe[:, c, cols], rhs=Sbf[:, 0, c, :], start=False, stop=False,
                      reads=["Qe%d" % c, "Sbf0"], writes=["ps%d" % bo2])
                    A("pe", "matmul", oreg, lhsT=Qo[:, c, cols], rhs=Sbf[:, 1, c, :], start=False, stop=True,
                      reads=["Qo%d" % c, "Sbf1"], writes=["ps%d" % bo2])
                bkv_o = kv_mm(Kho, KHO)
                ssa = pst[:, 24:25]
                rsa = pst[:, 25:26]
                an = ["attn%d_0" % sb_, "attn%d_1" % sb_]
                A("act", "activation", out=osq, in_=attn[:, sb_, :], func=AF.Square, accum_out=ssa,
                  reads=an, writes=["osq", "ssa%d" % par])
                ssh = pst[:, 26:30]
                A("act", "activation", out=osq2, in_=psb[bo2], func=AF.Square, reads=["ps%d" % bo2], writes=["osq2"])
                state_upd(bkv_o, 2 * s + 1, 0)
                def epi_tail(s=s, sb_=sb_, bo2=bo2, an=an, ssa=ssa, rsa=rsa, ssh=ssh):
                    A("pool", "tensor_scalar", out=ssa, in0=ssa, scalar1=1.0 / 512, scalar2=EPS, op0=ALU.mult, op1=ALU.add,
                      reads=["ssa%d" % par], writes=["ssa%d" % par])
                    A("pool", "tensor_tensor", out=rsa, in0=ssa, in1=neghalf[:, 0:1], op=ALU.pow,
                      reads=["ssa%d" % par, "neghalf"], writes=["rsa%d" % par])
                    A("pool", "tensor_scalar", out=attn_n[:, s, :], in0=attn[:, sb_, :], scalar1=rsa, scalar2=1.0,
                      op0=ALU.mult, op1=ALU.mult, reads=an + ["rsa%d" % par], writes=["attn_n%d" % s])
                    A("dve", "tensor_reduce", out=ssh, in_=osq2.rearrange("p (c v) -> p c v", c=4), axis=AX.X, op=ALU.add,
                      reads=["osq2"], writes=["ssh%d" % par])
                    A("pool", "tensor_scalar", out=ssh, in0=ssh, scalar1=1.0 / 128, scalar2=EPS, op0=ALU.mult, op1=ALU.add,
                      reads=["ssh%d" % par], writes=["ssh%d" % par])
                    A("pool", "tensor_tensor", out=ssh, in0=ssh, in1=neghalf[:, 0:4], op=ALU.pow,
                      reads=["ssh%d" % par, "neghalf"], writes=["ssh%d" % par])
                    A("dve", "tensor_tensor", out=rec_n[:, s, :].rearrange("p (c v) -> p c v", c=4),
                      in0=psb[bo2].rearrange("p (c v) -> p c v", c=4), in1=ssh.unsqueeze(2).to_broadcast([128, 4, 128]),
                      op=ALU.mult, reads=["ps%d" % bo2, "ssh%d" % par], writes=["rec_n%d" % s])
                pending_tail.append(epi_tail)
            while pending_tail:
                pending_tail.pop(0)()

            for c in range(8):
                bk = bank()
                pb = psb[bk].bitcast(BF16)
                src = attn_n if c < 4 else rec_n
                sname = "attn_n" if c < 4 else "rec_n"
                cc = c % 4
                for s in range(4):
                    A("pe", "transpose", out=pb[:, s * 128:(s + 1) * 128], in_=src[:, s, cc * 128:(cc + 1) * 128],
                      identity=ident, reads=[sname + "%d" % s, "ident"], writes=["ps%d" % bk])
                if c < 4:
                    A("act", "activation", out=mixT[:, c, :], in_=pb[:, 0:512], func=AF.Identity,
                      scale=smallF[:, SF_GA + c:SF_GA + c + 1], reads=["ps%d" % bk, "smallF"], writes=["mixT%d" % c])
                else:
                    A("dve", "scalar_tensor_tensor", out=mixT[:, c, :], in0=pb[:, 0:512],
                      scalar=smallF[:, SF_GH + cc:SF_GH + cc + 1], in1=silug[:, cc, :], op0=ALU.mult, op1=ALU.mult,
                      reads=["ps%d" % bk, "smallF", "silug%d" % cc], writes=["mixT%d" % c])

            def proj_residual(unit0, nK, src, srcname, gi, tmpbuf, tmpname):
                for nh in range(2):
                    bks_ = [bank() for _ in range(4)]
                    for u in range(nK // 2):
                        slot_ = load_unit(unit0 + nh * (nK // 2) + u)
                        for kk in range(2):
                            kc = 2 * u + kk
                            for s in range(4):
                                A("pe", "matmul", psb[bks_[s]], lhsT=src[:, kc, s * 128:(s + 1) * 128],
                                  rhs=ring[:, slot_, kk * 512:(kk + 1) * 512], start=(kc == 0), stop=(kc == nK - 1),
                                  reads=["rg%d" % slot_, srcname + "%d" % kc], writes=["ps%d" % bks_[s]])
                    for s in range(4):
                        sl = (g % 2) * 4 + s
                        tb = s % 2
                        A("dve", "tensor_tensor", out=tmpbuf[:, tb, :], in0=psb[bks_[s]],
                          in1=gates[:, b, gi, nh * 512:(nh + 1) * 512], op=ALU.mult,
                          reads=["ps%d" % bks_[s], "gates"], writes=tmpname(tb))
                        A("pool", "tensor_tensor", out=xbuf[:, sl, nh * 512:(nh + 1) * 512],
                          in0=xbuf[:, sl, nh * 512:(nh + 1) * 512], in1=tmpbuf[:, tb, :], op=ALU.add,
                          reads=["x%d" % sl] + tmpname(tb), writes=["x%d" % sl])

            if g == 0:
                cast(8)
                cast(9)
            proj_residual(22, 8, mixT, "mixT", 0, otmp, lambda tb: ["otmp%d" % tb])

            norm_stats(g, 1)
            norm_trans(b, af, modT[:, 2, :, :])
            P.barrier(ARENA_ENGS)

            def ffn_a(j):
                ub = j % 2
                if g == 0 and j in (0, 6):
                    cast(10 if j == 0 else 11)
                slot_u = load_unit(30 + 2 * j)
                bu = bank()
                for kc in range(8):
                    A("pe", "matmul", psb[bu], lhsT=ring[:, slot_u, kc * 128:(kc + 1) * 128], rhs=hT[:, kc, :],
                      start=(kc == 0), stop=(kc == 7), reads=["rg%d" % slot_u] + HTALL, writes=["ps%d" % bu])
                slot_v = load_unit(31 + 2 * j)
                bv = bank()
                for kc in range(8):
                    A("pe", "matmul", psb[bv], lhsT=ring[:, slot_v, kc * 128:(kc + 1) * 128], rhs=hT[:, kc, :],
                      start=(kc == 0), stop=(kc == 7), reads=["rg%d" % slot_v] + HTALL, writes=["ps%d" % bv])
                un = "ubuf%d" % ub
                A("pool", "tensor_copy", out=ubuf[:, ub, 0:2], in_=uprev[:, j, :], reads=["uprev%d" % j], writes=[un + "h"])
                A("act", "activation", out=ubuf[:, ub, 2:514], in_=psb[bu], func=AF.Identity,
                  reads=["ps%d" % bu], writes=[un])
                A("pool", "tensor_copy", out=uprev[:, j, :], in_=ubuf[:, ub, 512:514], reads=[un], writes=["uprev%d" % j])
                cw = lambda k, j=j: smallF[:, SF_CW + k * NFF + j:SF_CW + k * NFF + j + 1]
                A("act", "activation", out=psb[bu], in_=psb[bu], func=AF.Identity, scale=cw(2),
                  bias=smallF[:, SF_CB + j:SF_CB + j + 1], reads=["ps%d" % bu, "smallF"], writes=["ps%d" % bu])
                A("dve", "scalar_tensor_tensor", out=psb[bu], in0=ubuf[:, ub, 1:513], scalar=cw(1), in1=psb[bu],
                  op0=ALU.mult, op1=ALU.add, reads=[un, un + "h", "ps%d" % bu, "smallF"], writes=["ps%d" % bu])
                A("dve", "scalar_tensor_tensor", out=sgb[:, ub, :], in0=ubuf[:, ub, 0:512], scalar=cw(0), in1=psb[bu],
                  op0=ALU.mult, op1=ALU.add, reads=[un, un + "h", "ps%d" % bu, "smallF"], writes=["sgb%d" % ub])
                return bv

            def ffn_b(j, bv):
                ub = j % 2
                A("act", "activation", out=sgb[:, ub, :], in_=sgb[:, ub, :], func=AF.Silu,
                  reads=["sgb%d" % ub], writes=["sgb%d" % ub])
                A("dve", "tensor_tensor", out=gT[:, j, :], in0=sgb[:, ub, :], in1=psb[bv], op=ALU.mult,
                  reads=["sgb%d" % ub, "ps%d" % bv], writes=["gT%d" % j])

            bv_prev = None
            for j in range(NFF):
                bv_j = ffn_a(j)
                if j >= 1:
                    ffn_b(j - 1, bv_prev)
                bv_prev = bv_j
            ffn_b(NFF - 1, bv_prev)

            if g + 1 < NTT:
                norm_stats(g + 1, 0)
            proj_residual(74, NFF, gT, "gT", 1, dtmp, lambda tb: ["dtmp%d" % tb])
            if g + 1 < NTT:
                norm_trans((g + 1) // NT, am, modT[:, 0, :, :])

            for s in range(4):
                sl = (g % 2) * 4 + s
                ybi = s % 2
                ssf = pst[:, 30:31]
                rsf = pst[:, 31:32]
                A("act", "activation", out=fjunk, in_=xbuf[:, sl, :], func=AF.Square, accum_out=ssf,
                  reads=["x%d" % sl], writes=["fjunk", "ssf%d" % par])
                A("pool", "tensor_scalar", out=ssf, in0=ssf, scalar1=1.0 / D, scalar2=EPS, op0=ALU.mult, op1=ALU.add,
                  reads=["ssf%d" % par], writes=["ssf%d" % par])
                A("pool", "tensor_tensor", out=rsf, in0=ssf, in1=neghalf[:, 0:1], op=ALU.pow,
                  reads=["ssf%d" % par, "neghalf"], writes=["rsf%d" % par])
                A("dve", "scalar_tensor_tensor", out=yb[:, ybi, :], in0=xbuf[:, sl, :], scalar=rsf,
                  in1=rowbc[:, RF_FG:RF_FG + D], op0=ALU.mult, op1=ALU.mult,
                  reads=["x%d" % sl, "rsf%d" % par, "rowbc"], writes=["yb%d" % ybi])
                r0 = g * 512 + s * 128
                A("pool", "dma_start", out=yout_d[r0:r0 + 128, :], in_=yb[:, ybi, :], reads=["yb%d" % ybi],
                  dma="yo%d" % ybi)
        P.barrier()

        nw = P.emit(lambda e, ep: sems[(e, ep)], dsem)
        if dbg:
            print("ops", len(P.ops), "waits", nw, "maxcnt", P.maxcnt, "arena mixer", mixer_top, "ffn", FA.off, "pro", PA.off)
    return nc


def _weight_units(w_in, w_out, w_up, w_down):
    U = np.empty((NUNITS, 128, 1024), np.float32)

    def fm_unit(W, cols):
        return W[:, cols].reshape(8, 128, 128).transpose(1, 0, 2).reshape(128, 1024)

    def tm_unit(W, k0, cols):
        return W[k0 * 128:(k0 + 2) * 128][:, cols].reshape(2, 128, 512).transpose(1, 0, 2).reshape(128, 1024)

    ar = np.arange
    u = 0
    for c in range(4):
        U[u] = fm_unit(w_in, 1280 + c * 128 + ar(128)); u += 1
    for c in range(4):
        U[u] = fm_unit(w_in, 768 + c * 128 + ar(128)); u += 1
    for c in range(4):
        U[u] = fm_unit(w_in, 2304 + c * 128 + ar(128)); u += 1
    U[u] = fm_unit(w_in, 512 + ar(128)); u += 1
    for gq in range(4):
        cols = np.concatenate([gq * 64 + ar(64), (4 + gq) * 64 + ar(64)])
        U[u] = fm_unit(w_in, cols); u += 1
    for k in range(4):
        U[u] = tm_unit(w_in, 2 * k, 1792 + ar(512)); u += 1
    U[u] = fm_unit(w_in, 640 + ar(128)); u += 1
    assert u == 22
    for nh in range(2):
        for k in range(4):
            U[u] = tm_unit(w_out, 2 * k, nh * 512 + ar(512)); u += 1
    assert u == 30
    for j in range(NFF):
        U[u] = fm_unit(w_up, j * 128 + ar(128)); u += 1
        U[u] = fm_unit(w_up, DFF + j * 128 + ar(128)); u += 1
    assert u == 74
    for nh in range(2):
        for jp in range(11):
            U[u] = tm_unit(w_down, 2 * jp, nh * 512 + ar(512)); u += 1
    assert u == NUNITS
    return np.ascontiguousarray(U[UORDER]).reshape(NUNITS * 128, 1024)


def _host_prep(inp, T, NSEQ, n_cores):
    f = lambda a: np.ascontiguousarray(np.asarray(a, dtype=np.float32))
    w_in, w_out, w_up, w_down = f(inp["w_in"])[0], f(inp["w_out"])[0], f(inp["w_up"])[0], f(inp["w_down"])[0]
    wun = _weight_units(w_in, w_out, w_up, w_down)
    ada_w = f(inp["ada_w"])[0]
    adaun = np.ascontiguousarray(ada_w.reshape(8, 128, 12, 512).transpose(2, 1, 0, 3))
    adab = f(inp["ada_b"])[0]
    b_in = f(inp["b_in"])[0]
    sF = np.zeros((128, NSF), np.float32)
    fm_cols = ([1280 + c * 128 for c in range(4)] + [768 + c * 128 for c in range(4)] +
               [2304 + c * 128 for c in range(4)] + [512])
    for i, c0 in enumerate(fm_cols):
        sF[:, SF_BFM + i] = b_in[c0:c0 + 128]
    for gq in range(4):
        sF[0:64, SF_BFM + 13 + gq] = b_in[gq * 64:(gq + 1) * 64]
        sF[64:128, SF_BFM + 13 + gq] = b_in[(4 + gq) * 64:(5 + gq) * 64]
    sF[:, SF_GMIX:SF_GMIX + 8] = f(inp["mix_norm_g"])[0].reshape(8, 128).T
    sF[:, SF_GFFN:SF_GFFN + 8] = f(inp["ffn_norm_g"])[0].reshape(8, 128).T
    sF[:, SF_GA:SF_GA + 4] = f(inp["attn_out_g"])[0].reshape(4, 128).T
    sF[:, SF_GH:SF_GH + 4] = f(inp["hgrn_out_g"])[0].reshape(4, 128).T
    cw = f(inp["conv_w"])[0]
    for k in range(3):
        sF[:, SF_CW + k * NFF:SF_CW + (k + 1) * NFF] = cw[k].reshape(NFF, 128).T
    sF[:, SF_CB:SF_CB + NFF] = f(inp["conv_b"])[0].reshape(NFF, 128).T
    lbl = f(inp["hgrn_lb_logits"])
    sF[:, SF_L0:SF_L0 + 4] = lbl[0].reshape(4, 128).T
    sF[:, SF_L1:SF_L1 + 4] = lbl[1].reshape(4, 128).T
    rowF = np.zeros(NROW, np.float32)
    rowF[RF_FG:RF_FG + D] = f(inp["final_norm_g"])
    rowF[RF_BIH:RF_BIH + 512] = b_in[1792:2304]
    rowF[RF_BVA:RF_BVA + 128] = b_in[640:768]
    rowF[RF_SINK:RF_SINK + 8] = f(inp["attn_sinks"])[0]
    x = f(inp["x"])
    c = f(inp["c"])
    maps = []
    for i in range(n_cores):
        xs = x[i * NSEQ:(i + 1) * NSEQ].reshape(NSEQ * T, D)
        cs = c[i * NSEQ:(i + 1) * NSEQ]
        cTm = np.ascontiguousarray(cs.reshape(NSEQ, 8, 128).transpose(2, 1, 0).reshape(128, 8 * NSEQ))
        maps.append({"xin": np.ascontiguousarray(xs), "wun": wun, "adaun": adaun, "adab": adab, "cT": cTm,
                     "smallF": sF, "rowF": rowF})
    return maps


_NC_CACHE = {}


def kernel(**inputs):
    x = np.asarray(inputs["x"])
    B, T, _ = x.shape
    n_cores = 8 if B % 8 == 0 else 1
    NSEQ = B // n_cores
    key = (T, NSEQ)
    if key not in _NC_CACHE:
        _NC_CACHE[key] = build_nc(T, NSEQ)
    nc = _NC_CACHE[key]
    maps = _host_prep(inputs, T, NSEQ, n_cores)
    res = run_bass_kernel_spmd(nc, maps, core_ids=list(range(n_cores)))
    out = np.stack([np.asarray(r["y"]).reshape(NSEQ, T, D) for r in res.results], axis=0)
    return out.reshape(B, T, D).astype(np.float32)
```

```python
import numpy as np
from contextlib import ExitStack
import concourse.bass as bass
import concourse.mybir as mybir
from concourse.alu_op_type import AluOpType as ALU
from concourse.bass_utils import run_bass_kernel_spmd

F32 = mybir.dt.float32
BF16 = mybir.dt.bfloat16
AF = mybir.ActivationFunctionType
AX = mybir.AxisListType

ENGS = ("pe", "act", "dve", "pool", "sp")
D = 1024
DFF = 2816
NFF = 22
EPS = 1e-6
NUNITS = 96
RS = 8
NEG = -16384.0
_INPROJ_ORDER = [0, 4, 8, 12, 1, 5, 9, 13, 2, 6, 10, 14, 3, 7, 11, 15, 16, 17, 18, 19, 20, 21]
UORDER = _INPROJ_ORDER + list(range(22, NUNITS))
UPOS = {u: i for i, u in enumerate(UORDER)}

SF_BFM = 0
SF_GMIX = 17
SF_GFFN = 25
SF_GA = 33
SF_GH = 37
SF_CW = 41
SF_CB = 107
SF_L0 = 129
SF_L1 = 133
NSF = 137
RF_FG = 0
RF_BIH = 1024
RF_BVA = 1536
RF_SINK = 1664
NROW = 1672


class Prog:
    def __init__(self, nc):
        self.nc = nc
        self.ops = []
        self.last_w = {}
        self.readers = {}
        self.dma_readers = {}
        self.dma_last = {}
        self.eng_last = {e: None for e in ENGS}
        self.epoch = 0

    def op(self, eng, name, *args, reads=(), writes=(), dma=None, extra_deps=(), **kw):
        fn = None if name is None else (name, args, kw)
        idx = len(self.ops)
        deps = set(extra_deps)
        for r in reads:
            t = self.last_w.get(r)
            if t is not None:
                deps.add(t)
        for w in writes:
            t = self.last_w.get(w)
            if t is not None:
                deps.add(t)
            deps.update(self.readers.get(w, {}).values())
            deps.update(self.dma_readers.get(w, ()))
        if dma is not None:
            t = self.dma_last.get(dma)
            if t is not None:
                deps.add(t)
            self.dma_last[dma] = idx
        deps.discard(idx)
        self.ops.append(dict(eng=eng, fn=fn, deps=deps, dma=dma, epoch=self.epoch))
        for r in reads:
            if dma is not None:
                self.dma_readers.setdefault(r, []).append(idx)
            else:
                self.readers.setdefault(r, {})[eng] = idx
        for w in writes:
            self.last_w[w] = idx
            self.readers[w] = {}
            self.dma_readers[w] = []
        self.eng_last[eng] = idx
        return idx

    def barrier(self, engs=ENGS, skip_dma=()):
        deps = set(i for i in self.eng_last.values() if i is not None)
        deps |= set(v for k, v in self.dma_last.items() if not str(k).startswith(skip_dma or "\0"))
        for e in engs:
            self.op(e, None, extra_deps=deps)

    def emit(self, get_sem, dma_sems):
        nc = self.nc
        engobj = dict(pe=nc.tensor, act=nc.scalar, dve=nc.vector, pool=nc.gpsimd, sp=nc.sync)
        ops = self.ops

        def resolve(o):
            out = []
            stack = list(o["deps"])
            visited = set()
            while stack:
                d = stack.pop()
                if d in visited:
                    continue
                visited.add(d)
                od = ops[d]
                if od["fn"] is None:
                    stack.extend(od["deps"])
                    continue
                if od["dma"] is None and od["eng"] == "pe" and o["eng"] == "pe" and o["dma"] is None:
                    continue
                out.append(d)
            return out

        needed = set()
        for o in ops:
            o["rdeps"] = resolve(o)
            for d in o["rdeps"]:
                if ops[d]["dma"] is None:
                    needed.add(d)
        cnt = {}
        dcnt = {}
        for i, o in enumerate(ops):
            if o["fn"] is None:
                continue
            if o["dma"] is not None:
                dcnt[o["dma"]] = dcnt.get(o["dma"], 0) + 16
                o["tok"] = (("dma", o["dma"]), dcnt[o["dma"]])
            elif i in needed:
                k = ("eng", o["eng"], o["epoch"])
                cnt[k] = cnt.get(k, 0) + 1
                o["tok"] = (k, cnt[k])
        self.maxcnt = (max(cnt.values()) if cnt else 0, max(dcnt.values()) if dcnt else 0)
        seen = {e: {} for e in ENGS}
        nwaits = 0
        for i, o in enumerate(ops):
            E = o["eng"]
            eo = engobj[E]
            waits = {}
            for d in o["rdeps"]:
                key, val = ops[d]["tok"]
                if waits.get(key, 0) < val:
                    waits[key] = val
            for key, val in waits.items():
                if seen[E].get(key, 0) < val:
                    sem = dma_sems[key[1]] if key[0] == "dma" else get_sem(key[1], key[2])
                    eo.wait_ge(sem, val)
                    seen[E][key] = val
                    nwaits += 1
            if o["fn"] is None:
                continue
            nm, ar, kw = o["fn"]
            ins = getattr(eo, nm)(*ar, **kw)
            if o["dma"] is not None:
                ins.then_inc(dma_sems[o["dma"]], 16)
            elif i in needed:
                ins.then_inc(get_sem(E, o["epoch"]), 1)
        return nwaits


class Arena:
    def __init__(self, ap):
        self.ap = ap
        self.off = 0

    def take(self, shape, dt):
        n = int(np.prod(shape))
        esz = 4 if dt == F32 else 2
        nw = (n * esz + 3) // 4
        sl = self.ap[:, self.off:self.off + nw]
        self.off += nw
        assert self.off <= self.ap.shape[1], ("arena overflow", self.off, self.ap.shape)
        v = sl if dt == F32 else sl.bitcast(dt)
        if len(shape) == 1:
            return v
        if len(shape) == 2:
            return v.rearrange("p (a b) -> p a b", a=shape[0], b=shape[1])
        if len(shape) == 3:
            return v.rearrange("p (a b c) -> p a b c", a=shape[0], b=shape[1], c=shape[2])
        raise ValueError(shape)


def build_nc(T, NSEQ, dbg=False):
    NT = T // 512
    NTT = NSEQ * NT
    nc = bass.Bass("TRN2", target_bir_lowering=False)
    xin = nc.dram_tensor("xin", [NSEQ * T, D], F32, kind="ExternalInput").ap()
    wun = nc.dram_tensor("wun", [NUNITS * 128, 1024], F32, kind="ExternalInput").ap()
    adaun = nc.dram_tensor("adaun", [12, 128, 8, 512], F32, kind="ExternalInput").ap()
    adab = nc.dram_tensor("adab", [6144], F32, kind="ExternalInput").ap()
    cT = nc.dram_tensor("cT", [128, 8 * NSEQ], F32, kind="ExternalInput").ap()
    smallF_d = nc.dram_tensor("smallF", [128, NSF], F32, kind="ExternalInput").ap()
    rowF_d = nc.dram_tensor("rowF", [NROW], F32, kind="ExternalInput").ap()
    yout_d = nc.dram_tensor("y", [NSEQ * T, D], F32, kind="ExternalOutput").ap()
    wbf = nc.dram_tensor("wbf", [NUNITS * 128, 1024], BF16, kind="Internal").ap()

    with ExitStack() as es:
        def sb(name, shape, dt):
            return es.enter_context(nc.sbuf_tensor(name, shape, dt))[:]

        NEP = (NTT + 1) // 2 + 1
        sems = {(e, ep): es.enter_context(nc.semaphore("s_%s%d" % (e, ep)))
                for e in ("pe", "act", "dve", "pool") for ep in range(NEP)}
        NRG = (NTT + 3) // 4
        dma_keys = (["xl0", "xl1", "yo0", "yo1", "c0", "c1", "c2", "c2b", "c3", "ad0", "ad1"]
                    + ["rg%d_%d" % (i, q) for i in range(RS) for q in range(NRG)] + ["cast%d" % i for i in range(12)])
        dsem = {k: es.enter_context(nc.semaphore("d_" + k)) for k in dma_keys}
        P = Prog(nc)

        xbuf = sb("xbuf", [128, 8, D], F32)
        xn = sb("xn", [128, 4, D], BF16)
        hT = sb("hT", [128, 8, 512], BF16)
        yb = sb("yb", [128, 2, D], F32)
        ring = sb("ring", [128, RS, 1024], BF16)
        KaT = sb("KaT", [128, 2, 2, 512], BF16)
        Vaug = sb("Vaug", [128, 2, 4, 2, 65], BF16)
        Sf = sb("Sf", [128, 4, 128], F32)
        Sbf = sb("Sbf", [128, 2, 4, 128], BF16)
        uprev = sb("uprev", [128, NFF, 2], F32)
        Qe = sb("Qe", [128, 4, 512], BF16)
        Qo = sb("Qo", [128, 4, 512], BF16)
        Khe = sb("Khe", [128, 4, 4, 128], BF16)
        Kho = sb("Kho", [128, 4, 4, 128], BF16)
        ident = sb("ident", [128, 128], BF16)
        identf = sb("identf", [128, 128], F32)
        maskcur = sb("maskcur", [128, 4, 128], BF16)
        maskprev = sb("maskprev", [128, 4, 128], BF16)
        hmask = sb("hmask", [128, 4, 128], BF16)
        scanmask = sb("scanmask", [128, 512], F32)
        neghalf = sb("neghalf", [128, 8], F32)
        smallF = sb("smallFs", [128, NSF], F32)
        rowbc = sb("rowbc", [128, NROW], F32)
        gates = sb("gates", [128, NSEQ, 2, D], F32)
        modT = sb("modT", [128, 4, NSEQ, 8], F32)
        am = sb("am", [128, NSEQ, 8], F32)
        af = sb("af", [128, NSEQ, 8], F32)
        hc = sb("hc", [128, 8, 4], F32)
        expsink = sb("expsink", [128, 8], F32)
        el = sb("el", [128, 4, 8], F32)
        stat = sb("stat", [128, 2, 32], F32)
        arena_t = sb("arena", [128, 19968], F32)

        psb = [es.enter_context(nc.psum_tensor("ps%d" % i, [128, 512], F32))[:] for i in range(8)]
        bank_ctr = [0]

        def bank():
            k = bank_ctr[0] % 8
            bank_ctr[0] += 1
            return k

        MA = Arena(arena_t)
        PT = MA.take([2, 2, 512], BF16)
        attn = MA.take([2, 512], F32)
        attn_n = MA.take([4, 512], BF16)
        rec_n = MA.take([4, 512], BF16)
        Vh = MA.take([4, 512], BF16)
        ST = MA.take([2, 4, 128], BF16)
        osq = MA.take([512], F32)
        osq2 = MA.take([512], F32)
        mixT = MA.take([8, 512], BF16)
        otmp = MA.take([2, 512], F32)
        late_top = MA.off
        QaT = MA.take([4, 512], BF16)
        tA2 = MA.take([2, 512], F32)
        tB2 = MA.take([2, 512], F32)
        tC2 = MA.take([2, 512], F32)
        tD2 = MA.take([2, 512], F32)
        tE2 = MA.take([2, 512], F32)
        silug = MA.take([4, 512], F32)
        KtT = MA.take([4, 512], BF16)
        KhatT = MA.take([4, 512], BF16)
        mixer_top = MA.off
        FA = Arena(arena_t)
        gT = FA.take([NFF, 512], BF16)
        ubuf = FA.take([2, 516], F32)
        sgb = FA.take([2, 512], F32)
        dtmp = FA.take([2, 512], F32)
        fjunk = FA.take([1024], BF16)
        assert FA.off <= late_top, (FA.off, late_top)
        PA = Arena(arena_t)
        adab_bc = PA.take([2, 512], F32)
        adaf = PA.take([2, 8, 512], F32)
        adaw = PA.take([2, 8, 512], BF16)
        mtmp = PA.take([2, 512], F32)
        mtmp2 = PA.take([2, 512], F32)
        maskf = PA.take([4, 128], F32)
        cTs = PA.take([8 * NSEQ], F32)
        cond = PA.take([8 * NSEQ], F32)
        condrep = PA.take([8 * NSEQ, 128], BF16)

        A = P.op

        A("pool", "dma_start", out=smallF, in_=smallF_d, writes=["smallF"], dma="c0")
        A("pool", "dma_start", out=rowbc, in_=rowF_d.partition_broadcast(128), writes=["rowbc"], dma="c1")
        A("pool", "dma_start", out=cTs, in_=cT, writes=["cTs"], dma="c2")
        def cast(k):
            A("pool", "dma_start", out=wbf[k * 1024:(k + 1) * 1024, :], in_=wun[k * 1024:(k + 1) * 1024, :],
              reads=(["adaf0", "adaf1"] if k >= 2 else []), writes=["wbf%d" % k], dma="cast%d" % k)
        def xload(g):
            base = (g % 2) * 4
            r0 = g * 512
            A("pool", "dma_start", out=xbuf[:, base:base + 4, :],
                                            in_=xin[r0:r0 + 512, :].rearrange("(s p) d -> p s d", p=128),
              writes=["x%d" % (base + s) for s in range(4)], dma="xl%d" % (g % 2))
        xload(0)
        cast(0)
        cast(1)

        A("pool", "memset", identf, 0.0, writes=["identf"])
        A("pool", "affine_select", out=identf, in_=identf, pattern=[[-1, 128]], compare_op=ALU.not_equal,
                                            fill=1.0, base=0, channel_multiplier=1, reads=["identf"], writes=["identf"])
        A("dve", "tensor_copy", out=ident, in_=identf, reads=["identf"], writes=["ident"])
        A("pool", "memset", maskf, 0.0, writes=["maskf"])
        A("pool", "affine_select", out=maskf, in_=maskf, pattern=[[0, 4], [1, 128]], compare_op=ALU.is_ge,
                                            fill=NEG, base=0, channel_multiplier=-1, reads=["maskf"], writes=["maskf"])
        A("dve", "tensor_copy", out=maskcur, in_=maskf, reads=["maskf"], writes=["maskcur"])
        A("pool", "memset", maskf, 0.0, reads=["maskf"], writes=["maskf"])
        A("pool", "affine_select", out=maskf, in_=maskf, pattern=[[0, 4], [-1, 128]], compare_op=ALU.is_gt,
                                            fill=NEG, base=0, channel_multiplier=1, reads=["maskf"], writes=["maskf"])
        A("dve", "tensor_copy", out=maskprev, in_=maskf, reads=["maskf"], writes=["maskprev"])
        A("pool", "memset", maskf, 1.0, reads=["maskf"], writes=["maskf"])
        A("pool", "affine_select", out=maskf, in_=maskf, pattern=[[0, 4], [1, 128]], compare_op=ALU.is_ge,
                                            fill=0.0, base=0, channel_multiplier=-1, reads=["maskf"], writes=["maskf"])
        A("pool", "memset", maskf[0:64, :, 64:128], 0.0, reads=["maskf"], writes=["maskf"])
        A("dve", "tensor_copy", out=hmask, in_=maskf, reads=["maskf"], writes=["hmask"])
        A("pool", "memset", scanmask, 0.0, writes=["scanmask"])
        A("pool", "memset", scanmask.rearrange("p (c t) -> p c t", t=64)[:, :, 0:1], 1.0,
          reads=["scanmask"], writes=["scanmask"])
        A("pool", "memset", neghalf, -0.5, writes=["neghalf"])
        A("pool", "memset", KaT, 0.0, writes=["KaT0_0", "KaT0_1", "KaT1_0", "KaT1_1"])
        A("pool", "memset", Vaug, 1.0, writes=["Vaug0", "Vaug1"])
        A("pool", "memset", Qe, 0.0, writes=["Qe%d" % c for c in range(4)])
        A("pool", "memset", Qo, 0.0, writes=["Qo%d" % c for c in range(4)])
        A("pool", "memset", Khe, 0.0, writes=["Khe%d" % c for c in range(4)])
        A("pool", "memset", Kho, 0.0, writes=["Kho%d" % c for c in range(4)])

        lb, oml, homl, nhoml, lbh, halfbf = (hc[:, i, :] for i in range(6))
        A("dve", "tensor_tensor", out=hc[:, 6, :], in0=smallF[:, SF_L0:SF_L0 + 4], in1=smallF[:, SF_L1:SF_L1 + 4],
                                           op=ALU.subtract, reads=["smallF"], writes=["hc6"])
        A("act", "activation", out=lb, in_=hc[:, 6, :], func=AF.Sigmoid, reads=["hc6"], writes=["lb"])
        A("dve", "tensor_scalar", out=oml, in0=lb, scalar1=-1.0, scalar2=1.0, op0=ALU.mult, op1=ALU.add,
          reads=["lb"], writes=["oml"])
        A("dve", "tensor_scalar", out=homl, in0=oml, scalar1=0.5, scalar2=None, op0=ALU.mult,
          reads=["oml"], writes=["homl"])
        A("dve", "tensor_scalar", out=nhoml, in0=oml, scalar1=-0.5, scalar2=None, op0=ALU.mult,
          reads=["oml"], writes=["nhoml"])
        A("dve", "tensor_tensor", out=lbh, in0=lb, in1=homl, op=ALU.add, reads=["lb", "homl"], writes=["lbh"])
        A("dve", "tensor_scalar", out=halfbf, in0=smallF[:, SF_BFM:SF_BFM + 4], scalar1=0.5, scalar2=None,
                                           op0=ALU.mult, reads=["smallF"], writes=["halfbf"])
        HC = ["lb", "oml", "homl", "nhoml", "lbh", "halfbf"]
        A("act", "activation", out=expsink, in_=rowbc[:, RF_SINK:RF_SINK + 8], func=AF.Exp,
          reads=["rowbc"], writes=["expsink"])

        A("act", "activation", out=cond, in_=cTs, func=AF.Silu, reads=["cTs"], writes=["cond"])
        A("dve", "tensor_copy", out=condrep, in_=cond.unsqueeze(2).to_broadcast([128, 8 * NSEQ, 128]),
          reads=["cond"], writes=["condrep"])
        FIELD = {0: 0, 1: 1, 3: 2, 4: 3}
        for cb in range(12):
            ab = cb % 2
            A("sp", "dma_start", out=adaf[:, ab, :, :], in_=adaun[cb], writes=["adaf%d" % ab], dma="ad%d" % ab)
            A("pool", "dma_start", out=adab_bc[:, ab, :], in_=adab[cb * 512:(cb + 1) * 512].partition_broadcast(128),
              writes=["adab%d" % ab], dma="c3" if ab else "c2b")
            A("act", "activation", out=adaw[:, ab, 0:4, :], in_=adaf[:, ab, 0:4, :], func=AF.Identity,
              reads=["adaf%d" % ab], writes=["adaw%d" % ab])
            A("dve", "tensor_copy", out=adaw[:, ab, 4:8, :], in_=adaf[:, ab, 4:8, :],
              reads=["adaf%d" % ab], writes=["adaw%d_b" % ab])
            field, half = cb // 2, cb % 2
            for b in range(NSEQ):
                bk = bank()
                for kc in range(8):
                    A("pe", "matmul",
                        psb[bk], lhsT=condrep[:, kc * NSEQ + b, :], rhs=adaw[:, ab, kc, :],
                        start=(kc == 0), stop=(kc == 7),
                      reads=["condrep", "adaw%d" % ab, "adaw%d_b" % ab], writes=["ps%d" % bk])
                if field in (2, 5):
                    gi = 0 if field == 2 else 1
                    A("dve", "tensor_tensor",
                        out=gates[:, b, gi, half * 512:(half + 1) * 512], in0=psb[bk], in1=adab_bc[:, ab, :], op=ALU.add,
                      reads=["ps%d" % bk, "adab%d" % ab], writes=["gates"])
                else:
                    fi = FIELD[field]
                    mb = b
                    A("dve", "tensor_tensor",
                        out=mtmp[:, mb, :], in0=psb[bk], in1=adab_bc[:, ab, :], op=ALU.add,
                      reads=["ps%d" % bk, "adab%d" % ab], writes=["mtmp%d" % mb])
                    A("dve", "tensor_tensor",
                        out=mtmp2[:, mb, :].rearrange("p (a b) -> p a b", a=4),
                        in0=mtmp[:, mb, :].rearrange("p (a b) -> p a b", a=4),
                        in1=identf.unsqueeze(1).to_broadcast([128, 4, 128]), op=ALU.mult,
                      reads=["mtmp%d" % mb, "identf"], writes=["mtmp2%d" % mb])
                    A("dve", "tensor_reduce",
                        out=modT[:, fi, b, half * 4:(half + 1) * 4],
                        in_=mtmp2[:, mb, :].rearrange("p (a b) -> p a b", a=4), axis=AX.X, op=ALU.add,
                      reads=["mtmp2%d" % mb], writes=["modT"])
        cast(2)
        cast(3)
        for b in range(NSEQ):
            A("dve", "scalar_tensor_tensor", out=am[:, b, :], in0=modT[:, 1, b, :], scalar=1.0,
                                                           in1=smallF[:, SF_GMIX:SF_GMIX + 8], op0=ALU.add, op1=ALU.mult,
              reads=["modT", "smallF"], writes=["am"])
            A("dve", "scalar_tensor_tensor", out=af[:, b, :], in0=modT[:, 3, b, :], scalar=1.0,
                                                           in1=smallF[:, SF_GFFN:SF_GFFN + 8], op0=ALU.add, op1=ALU.mult,
              reads=["modT", "smallF"], writes=["af"])
        P.barrier(skip_dma=("cast",))

        unit_ctr = [0]
        cur_g = [0]

        def load_unit(uidx):
            slot = unit_ctr[0] % RS
            unit_ctr[0] += 1
            upos = UPOS[uidx]
            A("sp", "dma_start", out=ring[:, slot, :], in_=wbf[upos * 128:(upos + 1) * 128, :],
              reads=["wbf%d" % (upos // 8)], writes=["rg%d" % slot], dma="rg%d_%d" % (slot, cur_g[0] // 4))
            return slot

        def norm_stats(g, tagp):
            par = g % 2
            ss = stat[:, par, 0:4] if tagp == 0 else stat[:, par, 8:12]
            rs = stat[:, par, 4:8] if tagp == 0 else stat[:, par, 12:16]
            sn = "n%d_%d" % (tagp, par)
            for s in range(4):
                sl = (g % 2) * 4 + s
                A("act", "activation", out=xn[:, s, :], in_=xbuf[:, sl, :], func=AF.Square,
                  accum_out=ss[:, s:s + 1], reads=["x%d" % sl], writes=["xn%d" % s, sn + "ss%d" % s])
            A("pool", "tensor_scalar", out=ss, in0=ss, scalar1=1.0 / D, scalar2=EPS, op0=ALU.mult, op1=ALU.add,
              reads=[sn + "ss%d" % s for s in range(4)], writes=[sn + "ssb"])
            A("pool", "tensor_tensor", out=rs, in0=ss, in1=neghalf[:, 0:4], op=ALU.pow,
              reads=[sn + "ssb", "neghalf"] + [sn + "ss%d" % s for s in range(4)], writes=[sn + "rs"])
            for s in range(4):
                sl = (g % 2) * 4 + s
                if s % 2 == 0:
                    A("dve", "tensor_scalar", out=xn[:, s, :], in0=xbuf[:, sl, :], scalar1=rs[:, s:s + 1],
                      scalar2=None, op0=ALU.mult, reads=["x%d" % sl, sn + "rs"], writes=["xn%d" % s])
                else:
                    A("act", "activation", out=xn[:, s, :], in_=xbuf[:, sl, :], func=AF.Identity, scale=rs[:, s:s + 1],
                      reads=["x%d" % sl, sn + "rs"], writes=["xn%d" % s])

        def norm_trans(b, a_t, sh_ap):
            for c in range(8):
                bk = bank()
                pb = psb[bk].bitcast(BF16)
                for s in range(4):
                    A("pe", "transpose", out=pb[:, s * 128:(s + 1) * 128], in_=xn[:, s, c * 128:(c + 1) * 128],
                      identity=ident, reads=["xn%d" % s, "ident"], writes=["ps%d" % bk])
                if c % 2 == 0:
                    A("act", "activation", out=hT[:, c, :], in_=pb[:, 0:512], func=AF.Identity,
                      scale=a_t[:, b, c:c + 1], bias=sh_ap[:, b, c:c + 1],
                      reads=["ps%d" % bk, "am", "af", "modT"], writes=["hT%d" % c])
                else:
                    A("dve", "tensor_scalar", out=hT[:, c, :], in0=pb[:, 0:512], scalar1=a_t[:, b, c:c + 1],
                      scalar2=sh_ap[:, b, c:c + 1], op0=ALU.mult, op1=ALU.add,
                      reads=["ps%d" % bk, "am", "af", "modT"], writes=["hT%d" % c])

        HTALL = ["hT%d" % c for c in range(8)]

        def fm_chunk(uidx):
            slot = load_unit(uidx)
            bk = bank()
            for kc in range(8):
                A("pe", "matmul", psb[bk], lhsT=ring[:, slot, kc * 128:(kc + 1) * 128], rhs=hT[:, kc, :],
                  start=(kc == 0), stop=(kc == 7), reads=["rg%d" % slot] + HTALL, writes=["ps%d" % bk])
            return bk

        def bias(f):
            return smallF[:, SF_BFM + f:SF_BFM + f + 1]

        ARENA_ENGS = ("act", "dve", "pool")

        norm_stats(0, 0)
        norm_trans(0, am, modT[:, 0, :, :])
        for g in range(NTT):
            b = g // NT
            ti = g % NT
            par = g % 2
            cur_g[0] = g
            P.epoch = 1 + g // 2
            if g + 1 < NTT:
                xload(g + 1)
            if ti == 0:
                A("pool", "memset", Sf, 0.0, writes=["Sf0", "Sf1", "Sf2", "Sf3"])
                A("pool", "memset", Sbf[:, 0, :, :], 0.0, writes=["Sbf0"])
                A("pool", "memset", uprev, 0.0, writes=["uprev%d" % j for j in range(NFF)])

            def hg_pre(c):
                hp = c % 2
                tA, tB, tC, tD, tE = tA2[:, hp, :], tB2[:, hp, :], tC2[:, hp, :], tD2[:, hp, :], tE2[:, hp, :]
                bk = fm_chunk(c)
                A("act", "activation", out=tA, in_=psb[bk], func=AF.Tanh, scale=0.5, bias=halfbf[:, c:c + 1],
                  reads=["ps%d" % bk] + HC, writes=["tA%d" % hp])
                bk2 = fm_chunk(4 + c)
                A("act", "activation", out=tE, in_=psb[bk2], func=AF.Silu, bias=bias(4 + c),
                  reads=["ps%d" % bk2, "smallF"], writes=["tE%d" % hp])
                A("act", "activation", out=tB, in_=tA, func=AF.Identity, scale=nhoml[:, c:c + 1], bias=homl[:, c:c + 1],
                  reads=["tA%d" % hp] + HC, writes=["tB%d" % hp])
                A("act", "activation", out=tC, in_=tA, func=AF.Identity, scale=homl[:, c:c + 1], bias=lbh[:, c:c + 1],
                  reads=["tA%d" % hp] + HC, writes=["tC%d" % hp])
                A("dve", "tensor_tensor_scan", out=tD, data0=scanmask, data1=tC, initial=0.0, op0=ALU.max,
                  op1=ALU.mult, reads=["tC%d" % hp, "scanmask"], writes=["tD%d" % hp])
                A("dve", "reciprocal", out=tC, in_=tD, reads=["tD%d" % hp], writes=["tC%d" % hp])
                tE4 = tE.rearrange("p (j two t) -> p j two t", two=2, t=64)
                tD4 = tD.rearrange("p (j two t) -> p j two t", two=2, t=64)
                Qe4 = Qe[:, c, :].rearrange("p (j two t) -> p j two t", two=2, t=64)
                Qo4 = Qo[:, c, :].rearrange("p (j two t) -> p j two t", two=2, t=64)
                A("dve", "tensor_tensor", out=tB, in0=tB, in1=tC, op=ALU.mult, reads=["tB%d" % hp, "tC%d" % hp], writes=["tB%d" % hp])
                tB3 = tB.rearrange("p (j t) -> p j t", t=64)
                tD3 = tD.rearrange("p (j t) -> p j t", t=64)
                A("dve", "tensor_tensor", out=KhatT[:, c, :].rearrange("p (j t) -> p j t", t=64), in0=tB3,
                  in1=tD3[:, :, 63:64].to_broadcast([128, 8, 64]), op=ALU.mult,
                  reads=["tB%d" % hp, "tD%d" % hp], writes=["KhatT%d" % c])
                A("dve", "tensor_tensor", out=Qe4[:, :, 0, :], in0=tE4[:, :, 0, :], in1=tD4[:, :, 0, :], op=ALU.mult,
                  reads=["tE%d" % hp, "tD%d" % hp], writes=["Qe%d" % c])
                A("dve", "tensor_tensor", out=Qo4[:, :, 1, :], in0=tE4[:, :, 1, :], in1=tD4[:, :, 1, :], op=ALU.mult,
                  reads=["tE%d" % hp, "tD%d" % hp], writes=["Qo%d" % c])
                A("pool", "tensor_copy", out=KtT[:, c, :], in_=tB, reads=["tB%d" % hp], writes=["KtT%d" % c])
                A("pool", "tensor_copy", out=el[:, c, :], in_=tD3[:, :, 63], reads=["tD%d" % hp], writes=["el%d" % c])

            def hg_trans(c):
                bk3 = bank()
                pb = psb[bk3].bitcast(BF16)
                for s in range(4):
                    A("pe", "transpose", out=pb[:, s * 128:(s + 1) * 128], in_=KhatT[:, c, s * 128:(s + 1) * 128],
                      identity=ident, reads=["KhatT%d" % c, "ident"], writes=["ps%d" % bk3])
                pb3 = pb[:, 0:512].rearrange("p (s k) -> p s k", s=4)
                A("act", "activation", out=Khe[0:64, :, c, :], in_=pb3[0:64], func=AF.Identity,
                  reads=["ps%d" % bk3], writes=["Khe%d" % c])
                A("act", "activation", out=Kho[64:128, :, c, :], in_=pb3[64:128], func=AF.Identity,
                  reads=["ps%d" % bk3], writes=["Kho%d" % c])

            def g_chunk(c):
                bk = fm_chunk(8 + c)
                A("act", "activation", out=silug[:, c, :], in_=psb[bk], func=AF.Silu, bias=bias(8 + c),
                  reads=["ps%d" % bk, "smallF"], writes=["silug%d" % c])

            def ka_chunk():
                bk = fm_chunk(12)
                A("act", "activation", out=KaT[0:64, par, 0, :], in_=psb[bk][0:64], func=AF.Identity,
                  bias=bias(12)[0:64], reads=["ps%d" % bk, "smallF"], writes=["KaT%d_0" % par])
                A("act", "activation", out=KaT[64:128, par, 1, :], in_=psb[bk][64:128], func=AF.Identity,
                  bias=bias(12)[64:128], reads=["ps%d" % bk, "smallF"], writes=["KaT%d_1" % par])

            def qa_chunk(gq):
                bk = fm_chunk(13 + gq)
                A("act", "activation", out=QaT[:, gq, :], in_=psb[bk], func=AF.Identity, bias=bias(13 + gq),
                  reads=["ps%d" % bk, "smallF"], writes=["QaT%d" % gq])

            def ih_group():
                bks = [bank() for _ in range(4)]
                for u in range(4):
                    slot = load_unit(17 + u)
                    for kk in range(2):
                        kc = 2 * u + kk
                        for s in range(4):
                            A("pe", "matmul", psb[bks[s]], lhsT=hT[:, kc, s * 128:(s + 1) * 128],
                              rhs=ring[:, slot, kk * 512:(kk + 1) * 512], start=(kc == 0), stop=(kc == 7),
                              reads=["rg%d" % slot, "hT%d" % kc], writes=["ps%d" % bks[s]])
                for s in range(4):
                    A("dve", "tensor_tensor", out=Vh[:, s, :], in0=psb[bks[s]], in1=rowbc[:, RF_BIH:RF_BIH + 512],
                      op=ALU.add, reads=["ps%d" % bks[s], "rowbc"], writes=["Vh%d" % s])

            def va_group():
                slot = load_unit(21)
                bk = bank()
                for s in range(4):
                    for kc in range(8):
                        A("pe", "matmul", psb[bk][:, s * 128:(s + 1) * 128], lhsT=hT[:, kc, s * 128:(s + 1) * 128],
                          rhs=ring[:, slot, kc * 128:(kc + 1) * 128], start=(kc == 0), stop=(kc == 7),
                          reads=["rg%d" % slot, "hT%d" % kc], writes=["ps%d" % bk])
                A("dve", "tensor_tensor", out=Vaug[:, par, :, :, 0:64],
                  in0=psb[bk].rearrange("p (s h d) -> p s h d", s=4, h=2),
                  in1=rowbc[:, RF_BVA:RF_BVA + 128].rearrange("p (h d) -> p h d", h=2).unsqueeze(1).to_broadcast([128, 4, 2, 64]),
                  op=ALU.add, reads=["ps%d" % bk, "rowbc"], writes=["Vaug%d" % par])

            hg_pre(0); g_chunk(0); ka_chunk()
            if g == 0:
                cast(4)
            hg_pre(1); g_chunk(1); qa_chunk(0)
            hg_pre(2); g_chunk(2); qa_chunk(1); hg_trans(0)
            if g == 0:
                cast(5)
            hg_pre(3); g_chunk(3); qa_chunk(2); hg_trans(1)
            qa_chunk(3)
            if g > 0:
                P.barrier(ARENA_ENGS, skip_dma=("cast",))
            ih_group(); hg_trans(2); va_group(); hg_trans(3)
            if g == 0:
                cast(6)
                cast(7)

            KTT = ["KtT%d" % c for c in range(4)]
            QE = ["Qe%d" % c for c in range(4)]
            QO = ["Qo%d" % c for c in range(4)]
            KHE = ["Khe%d" % c for c in range(4)]
            KHO = ["Kho%d" % c for c in range(4)]
            EL = ["el%d" % c for c in range(4)]
            QAT = ["QaT%d" % c for c in range(4)]

            pst = stat[:, par, :]
            pending_tail = []
            for s in range(4):
                first = (ti == 0 and s == 0)
                cols = slice(s * 128, (s + 1) * 128)
                sb_ = s % 2
                bsc = bank()
                for c in range(4):
                    A("pe", "matmul", psb[bsc][:, c * 128:(c + 1) * 128], lhsT=KtT[:, c, cols], rhs=Qe[:, c, cols],
                      start=True, stop=False, reads=["KtT%d" % c, "Qe%d" % c], writes=["ps%d" % bsc])
                    A("pe", "matmul", psb[bsc][:, c * 128:(c + 1) * 128], lhsT=KtT[:, c, cols], rhs=Qo[:, c, cols],
                      start=False, stop=True, reads=["KtT%d" % c, "Qo%d" % c], writes=["ps%d" % bsc])

                def kv_mm(khat, khnames):
                    bkv = bank()
                    for c in range(4):
                        A("pe", "matmul", psb[bkv][:, c * 128:(c + 1) * 128], lhsT=khat[:, s, c, :],
                          rhs=Vh[:, s, c * 128:(c + 1) * 128], start=True, stop=True,
                          reads=[khnames[c], "Vh%d" % s], writes=["ps%d" % bkv])
                    return bkv

                def state_upd(bkv, j, dst):
                    for c in range(4):
                        A("dve", "scalar_tensor_tensor", out=Sf[:, c, :], in0=Sf[:, c, :], scalar=el[:, c, j:j + 1],
                          in1=psb[bkv][:, c * 128:(c + 1) * 128], op0=ALU.mult, op1=ALU.add,
                          reads=["Sf%d" % c, "el%d" % c, "ps%d" % bkv], writes=["Sf%d" % c])
                    A("act", "activation", out=Sbf[:, dst, :, :], in_=Sf, func=AF.Identity,
                      reads=["Sf0", "Sf1", "Sf2", "Sf3"], writes=["Sbf%d" % dst])

                bkv_e = kv_mm(Khe, KHE)
                prevs = {}
                for h in range(2):
                    bc_ = bank()
                    A("pe", "matmul", psb[bc_], lhsT=KaT[:, par, h, cols], rhs=QaT[:, :, cols], start=True, stop=False,
                      reads=["KaT%d_%d" % (par, h)] + QAT, writes=["ps%d" % bc_])
                    A("pe", "matmul", psb[bc_], lhsT=ident, rhs=maskcur.rearrange("p a b -> p (a b)"),
                      start=False, stop=True, reads=["ident", "maskcur"], writes=["ps%d" % bc_])
                    A("act", "activation", out=PT[:, h, 1, :], in_=psb[bc_], func=AF.Exp, scale=0.125,
                      reads=["ps%d" % bc_], writes=["PTc%d" % h])
                    if not first:
                        bp_ = bank()
                        if s > 0:
                            kprev = KaT[:, par, h, (s - 1) * 128:s * 128]
                            kname = "KaT%d_%d" % (par, h)
                            vprev = Vaug[:, par, s - 1, h, :]
                            vname = "Vaug%d" % par
                        else:
                            kprev = KaT[:, 1 - par, h, 384:512]
                            kname = "KaT%d_%d" % (1 - par, h)
                            vprev = Vaug[:, 1 - par, 3, h, :]
                            vname = "Vaug%d" % (1 - par)
                        prevs[h] = (vprev, vname)
                        A("pe", "matmul", psb[bp_], lhsT=kprev, rhs=QaT[:, :, cols], start=True, stop=False,
                          reads=[kname] + QAT, writes=["ps%d" % bp_])
                        A("pe", "matmul", psb[bp_], lhsT=ident, rhs=maskprev.rearrange("p a b -> p (a b)"),
                          start=False, stop=True, reads=["ident", "maskprev"], writes=["ps%d" % bp_])
                        A("act", "activation", out=PT[:, h, 0, :], in_=psb[bp_], func=AF.Exp, scale=0.125,
                          reads=["ps%d" % bp_], writes=["PTp%d" % h])
                A("dve", "tensor_tensor", out=ST[:, sb_, :, :], in0=psb[bsc].rearrange("p (c t) -> p c t", c=4),
                  in1=hmask, op=ALU.mult, reads=["ps%d" % bsc, "hmask"], writes=["ST%d" % sb_])
                state_upd(bkv_e, 2 * s, 1)
                while pending_tail:
                    pending_tail.pop(0)()
                if s > 0:
                    bank()
                for h in range(2):
                    bo_ = bank()
                    for gq in range(4):
                        oreg = psb[bo_][:, gq * 65:(gq + 1) * 65]
                        A("pe", "matmul", oreg, lhsT=PT[:, h, 1, gq * 128:(gq + 1) * 128], rhs=Vaug[:, par, s, h, :],
                          start=True, stop=first, reads=["PTc%d" % h, "Vaug%d" % par], writes=["ps%d" % bo_])
                        if not first:
                            A("pe", "matmul", oreg, lhsT=PT[:, h, 0, gq * 128:(gq + 1) * 128], rhs=prevs[h][0],
                              start=False, stop=True, reads=["PTp%d" % h, prevs[h][1]], writes=["ps%d" % bo_])
                    o3 = psb[bo_][:, 0:260].rearrange("p (g d) -> p g d", g=4)
                    den = pst[:, 16 + h * 4:20 + h * 4]
                    dn = "den%d%d" % (par, h)
                    A("dve", "tensor_tensor", out=den, in0=o3[:, :, 64], in1=expsink[:, h * 4:(h + 1) * 4], op=ALU.add,
                      reads=["ps%d" % bo_, "expsink"], writes=[dn])
                    A("dve", "reciprocal", out=den, in_=den, reads=[dn], writes=[dn])
                    A("dve", "tensor_tensor", out=attn[:, sb_, h * 256:(h + 1) * 256].rearrange("p (g d) -> p g d", g=4),
                      in0=o3[:, :, 0:64], in1=den.unsqueeze(2).to_broadcast([128, 4, 64]), op=ALU.mult,
                      reads=["ps%d" % bo_, dn], writes=["attn%d_%d" % (sb_, h)])
                bo2 = bank()
                for c in range(4):
                    oreg = psb[bo2][:, c * 128:(c + 1) * 128]
                    A("pe", "matmul", oreg, lhsT=ST[:, sb_, c, :], rhs=Vh[:, s, c * 128:(c + 1) * 128],
                      start=True, stop=False, reads=["ST%d" % sb_, "Vh%d" % s], writes=["ps%d" % bo2])
                    A("pe", "matmul", oreg, lhsT=Qe[:, c, cols], rhs=Sbf[:, 0, c, :], start=False, stop=False,
                      reads=["Qe%d" % c, "Sbf0"], writes=["ps%d" % bo2])
                    A("pe", "matmul", oreg, lhsT=Qo[:, c, cols], rhs=Sbf[:, 1, c, :], start=False, stop=True,
                      reads=["Qo%d" % c, "Sbf1"], writes=["ps%d" % bo2])
                bkv_o = kv_mm(Kho, KHO)
                ssa = pst[:, 24:25]
                rsa = pst[:, 25:26]
                an = ["attn%d_0" % sb_, "attn%d_1" % sb_]
                A("act", "activation", out=osq, in_=attn[:, sb_, :], func=AF.Square, accum_out=ssa,
                  reads=an, writes=["osq", "ssa%d" % par])
                ssh = pst[:, 26:30]
                A("act", "activation", out=osq2, in_=psb[bo2], func=AF.Square, reads=["ps%d" % bo2], writes=["osq2"])
                state_upd(bkv_o, 2 * s + 1, 0)
                def epi_tail(s=s, sb_=sb_, bo2=bo2, an=an, ssa=ssa, rsa=rsa, ssh=ssh):
                    A("pool", "tensor_scalar", out=ssa, in0=ssa, scalar1=1.0 / 512, scalar2=EPS, op0=ALU.mult, op1=ALU.add,
                      reads=["ssa%d" % par], writes=["ssa%d" % par])
                    A("pool", "tensor_tensor", out=rsa, in0=ssa, in1=neghalf[:, 0:1], op=ALU.pow,
                      reads=["ssa%d" % par, "neghalf"], writes=["rsa%d" % par])
                    A("pool", "tensor_scalar", out=attn_n[:, s, :], in0=attn[:, sb_, :], scalar1=rsa, scalar2=1.0,
                      op0=ALU.mult, op1=ALU.mult, reads=an + ["rsa%d" % par], writes=["attn_n%d" % s])
                    A("dve", "tensor_reduce", out=ssh, in_=osq2.rearrange("p (c v) -> p c v", c=4), axis=AX.X, op=ALU.add,
                      reads=["osq2"], writes=["ssh%d" % par])
                    A("pool", "tensor_scalar", out=ssh, in0=ssh, scalar1=1.0 / 128, scalar2=EPS, op0=ALU.mult, op1=ALU.add,
                      reads=["ssh%d" % par], writes=["ssh%d" % par])
                    A("pool", "tensor_tensor", out=ssh, in0=ssh, in1=neghalf[:, 0:4], op=ALU.pow,
                      reads=["ssh%d" % par, "neghalf"], writes=["ssh%d" % par])
                    A("dve", "tensor_tensor", out=rec_n[:, s, :].rearrange("p (c v) -> p c v", c=4),
                      in0=psb[bo2].rearrange("p (c v) -> p c v", c=4), in1=ssh.unsqueeze(2).to_broadcast([128, 4, 128]),
                      op=ALU.mult, reads=["ps%d" % bo2, "ssh%d" % par], writes=["rec_n%d" % s])
                pending_tail.append(epi_tail)
            while pending_tail:
                pending_tail.pop(0)()

            mbanks = [bank() for _ in range(4)]
            mpb = [psb[bk].bitcast(BF16) for bk in mbanks]
            for sgrp in ((0, 1, 2), (3,)):
                for c in range(8):
                    src = attn_n if c < 4 else rec_n
                    sname = "attn_n" if c < 4 else "rec_n"
                    cc = c % 4
                    off = (c % 2) * 512
                    for s in sgrp:
                        A("pe", "transpose", out=mpb[c // 2][:, off + s * 128:off + (s + 1) * 128],
                          in_=src[:, s, cc * 128:(cc + 1) * 128], identity=ident,
                          reads=[sname + "%d" % s, "ident"], writes=["ps%d" % mbanks[c // 2]])
            for c in range(8):
                cc = c % 4
                bk = mbanks[c // 2]
                pbc = mpb[c // 2][:, (c % 2) * 512:(c % 2) * 512 + 512]
                if c < 4:
                    A("act", "activation", out=mixT[:, c, :], in_=pbc, func=AF.Identity,
                      scale=smallF[:, SF_GA + c:SF_GA + c + 1], reads=["ps%d" % bk, "smallF"], writes=["mixT%d" % c])
                else:
                    A("dve", "scalar_tensor_tensor", out=mixT[:, c, :], in0=pbc,
                      scalar=smallF[:, SF_GH + cc:SF_GH + cc + 1], in1=silug[:, cc, :], op0=ALU.mult, op1=ALU.mult,
                      reads=["ps%d" % bk, "smallF", "silug%d" % cc], writes=["mixT%d" % c])

            def proj_residual(unit0, nK, src, srcname, gi, tmpbuf, tmpname):
                for nh in range(2):
                    bks_ = [bank() for _ in range(4)]
                    for u in range(nK // 2):
                        slot_ = load_unit(unit0 + nh * (nK // 2) + u)
                        for kk in range(2):
                            kc = 2 * u + kk
                            for s in range(4):
                                A("pe", "matmul", psb[bks_[s]], lhsT=src[:, kc, s * 128:(s + 1) * 128],
                                  rhs=ring[:, slot_, kk * 512:(kk + 1) * 512], start=(kc == 0), stop=(kc == nK - 1),
                                  reads=["rg%d" % slot_, srcname + "%d" % kc], writes=["ps%d" % bks_[s]])
                    for s in range(4):
                        sl = (g % 2) * 4 + s
                        tb = s % 2
                        A("dve", "tensor_tensor", out=tmpbuf[:, tb, :], in0=psb[bks_[s]],
                          in1=gates[:, b, gi, nh * 512:(nh + 1) * 512], op=ALU.mult,
                          reads=["ps%d" % bks_[s], "gates"], writes=tmpname(tb))
                        A("pool", "tensor_tensor", out=xbuf[:, sl, nh * 512:(nh + 1) * 512],
                          in0=xbuf[:, sl, nh * 512:(nh + 1) * 512], in1=tmpbuf[:, tb, :], op=ALU.add,
                          reads=["x%d" % sl] + tmpname(tb), writes=["x%d" % sl])

            if g == 0:
                cast(8)
                cast(9)
            proj_residual(22, 8, mixT, "mixT", 0, otmp, lambda tb: ["otmp%d" % tb])

            norm_stats(g, 1)
            norm_trans(b, af, modT[:, 2, :, :])
            P.barrier(ARENA_ENGS)

            def ffn_a(j):
                ub = j % 2
                if g == 0 and j in (0, 6):
                    cast(10 if j == 0 else 11)
                slot_u = load_unit(30 + 2 * j)
                bu = bank()
                for kc in range(8):
                    A("pe", "matmul", psb[bu], lhsT=ring[:, slot_u, kc * 128:(kc + 1) * 128], rhs=hT[:, kc, :],
                      start=(kc == 0), stop=(kc == 7), reads=["rg%d" % slot_u] + HTALL, writes=["ps%d" % bu])
                slot_v = load_unit(31 + 2 * j)
                bv = bank()
                for kc in range(8):
                    A("pe", "matmul", psb[bv], lhsT=ring[:, slot_v, kc * 128:(kc + 1) * 128], rhs=hT[:, kc, :],
                      start=(kc == 0), stop=(kc == 7), reads=["rg%d" % slot_v] + HTALL, writes=["ps%d" % bv])
                un = "ubuf%d" % ub
                A("pool", "tensor_copy", out=ubuf[:, ub, 0:2], in_=uprev[:, j, :], reads=["uprev%d" % j], writes=[un + "h"])
                A("act", "activation", out=ubuf[:, ub, 2:514], in_=psb[bu], func=AF.Identity,
                  reads=["ps%d" % bu], writes=[un])
                A("pool", "tensor_copy", out=uprev[:, j, :], in_=ubuf[:, ub, 512:514], reads=[un], writes=["uprev%d" % j])
                cw = lambda k, j=j: smallF[:, SF_CW + k * NFF + j:SF_CW + k * NFF + j + 1]
                A("act", "activation", out=psb[bu], in_=psb[bu], func=AF.Identity, scale=cw(2),
                  bias=smallF[:, SF_CB + j:SF_CB + j + 1], reads=["ps%d" % bu, "smallF"], writes=["ps%d" % bu])
                A("dve", "scalar_tensor_tensor", out=psb[bu], in0=ubuf[:, ub, 1:513], scalar=cw(1), in1=psb[bu],
                  op0=ALU.mult, op1=ALU.add, reads=[un, un + "h", "ps%d" % bu, "smallF"], writes=["ps%d" % bu])
                A("dve", "scalar_tensor_tensor", out=sgb[:, ub, :], in0=ubuf[:, ub, 0:512], scalar=cw(0), in1=psb[bu],
                  op0=ALU.mult, op1=ALU.add, reads=[un, un + "h", "ps%d" % bu, "smallF"], writes=["sgb%d" % ub])
                return bv

            def ffn_b(j, bv):
                ub = j % 2
                A("act", "activation", out=sgb[:, ub, :], in_=sgb[:, ub, :], func=AF.Silu,
                  reads=["sgb%d" % ub], writes=["sgb%d" % ub])
                A("dve", "tensor_tensor", out=gT[:, j, :], in0=sgb[:, ub, :], in1=psb[bv], op=ALU.mult,
                  reads=["sgb%d" % ub, "ps%d" % bv], writes=["gT%d" % j])

            bv_prev = None
            for j in range(NFF):
                bv_j = ffn_a(j)
                if j >= 1:
                    ffn_b(j - 1, bv_prev)
                bv_prev = bv_j
            ffn_b(NFF - 1, bv_prev)

            if g + 1 < NTT:
                norm_stats(g + 1, 0)
            proj_residual(74, NFF, gT, "gT", 1, dtmp, lambda tb: ["dtmp%d" % tb])
            if g + 1 < NTT:
                norm_trans((g + 1) // NT, am, modT[:, 0, :, :])

            for s in range(4):
                sl = (g % 2) * 4 + s
                ybi = s % 2
                ssf = pst[:, 30:31]
                rsf = pst[:, 31:32]
                A("act", "activation", out=fjunk, in_=xbuf[:, sl, :], func=AF.Square, accum_out=ssf,
                  reads=["x%d" % sl], writes=["fjunk", "ssf%d" % par])
                A("pool", "tensor_scalar", out=ssf, in0=ssf, scalar1=1.0 / D, scalar2=EPS, op0=ALU.mult, op1=ALU.add,
                  reads=["ssf%d" % par], writes=["ssf%d" % par])
                A("pool", "tensor_tensor", out=rsf, in0=ssf, in1=neghalf[:, 0:1], op=ALU.pow,
                  reads=["ssf%d" % par, "neghalf"], writes=["rsf%d" % par])
                A("dve", "scalar_tensor_tensor", out=yb[:, ybi, :], in0=xbuf[:, sl, :], scalar=rsf,
                  in1=rowbc[:, RF_FG:RF_FG + D], op0=ALU.mult, op1=ALU.mult,
                  reads=["x%d" % sl, "rsf%d" % par, "rowbc"], writes=["yb%d" % ybi])
                r0 = g * 512 + s * 128
                A("pool", "dma_start", out=yout_d[r0:r0 + 128, :], in_=yb[:, ybi, :], reads=["yb%d" % ybi],
                  dma="yo%d" % ybi)
        P.barrier()

        nw = P.emit(lambda e, ep: sems[(e, ep)], dsem)
        if dbg:
            print("ops", len(P.ops), "waits", nw, "maxcnt", P.maxcnt, "arena mixer", mixer_top, "ffn", FA.off, "pro", PA.off)
    return nc


def _weight_units(w_in, w_out, w_up, w_down):
    U = np.empty((NUNITS, 128, 1024), np.float32)

    def fm_unit(W, cols):
        return W[:, cols].reshape(8, 128, 128).transpose(1, 0, 2).reshape(128, 1024)

    def tm_unit(W, k0, cols):
        return W[k0 * 128:(k0 + 2) * 128][:, cols].reshape(2, 128, 512).transpose(1, 0, 2).reshape(128, 1024)

    ar = np.arange
    u = 0
    for c in range(4):
        U[u] = fm_unit(w_in, 1280 + c * 128 + ar(128)); u += 1
    for c in range(4):
        U[u] = fm_unit(w_in, 768 + c * 128 + ar(128)); u += 1
    for c in range(4):
        U[u] = fm_unit(w_in, 2304 + c * 128 + ar(128)); u += 1
    U[u] = fm_unit(w_in, 512 + ar(128)); u += 1
    for gq in range(4):
        cols = np.concatenate([gq * 64 + ar(64), (4 + gq) * 64 + ar(64)])
        U[u] = fm_unit(w_in, cols); u += 1
    for k in range(4):
        U[u] = tm_unit(w_in, 2 * k, 1792 + ar(512)); u += 1
    U[u] = fm_unit(w_in, 640 + ar(128)); u += 1
    assert u == 22
    for nh in range(2):
        for k in range(4):
            U[u] = tm_unit(w_out, 2 * k, nh * 512 + ar(512)); u += 1
    assert u == 30
    for j in range(NFF):
        U[u] = fm_unit(w_up, j * 128 + ar(128)); u += 1
        U[u] = fm_unit(w_up, DFF + j * 128 + ar(128)); u += 1
    assert u == 74
    for nh in range(2):
        for jp in range(11):
            U[u] = tm_unit(w_down, 2 * jp, nh * 512 + ar(512)); u += 1
    assert u == NUNITS
    return np.ascontiguousarray(U[UORDER]).reshape(NUNITS * 128, 1024)


def _host_prep(inp, T, NSEQ, n_cores):
    f = lambda a: np.ascontiguousarray(np.asarray(a, dtype=np.float32))
    w_in, w_out, w_up, w_down = f(inp["w_in"])[0], f(inp["w_out"])[0], f(inp["w_up"])[0], f(inp["w_down"])[0]
    wun = _weight_units(w_in, w_out, w_up, w_down)
    ada_w = f(inp["ada_w"])[0]
    adaun = np.ascontiguousarray(ada_w.reshape(8, 128, 12, 512).transpose(2, 1, 0, 3))
    adab = f(inp["ada_b"])[0]
    b_in = f(inp["b_in"])[0]
    sF = np.zeros((128, NSF), np.float32)
    fm_cols = ([1280 + c * 128 for c in range(4)] + [768 + c * 128 for c in range(4)] +
               [2304 + c * 128 for c in range(4)] + [512])
    for i, c0 in enumerate(fm_cols):
        sF[:, SF_BFM + i] = b_in[c0:c0 + 128]
    for gq in range(4):
        sF[0:64, SF_BFM + 13 + gq] = b_in[gq * 64:(gq + 1) * 64]
        sF[64:128, SF_BFM + 13 + gq] = b_in[(4 + gq) * 64:(5 + gq) * 64]
    sF[:, SF_GMIX:SF_GMIX + 8] = f(inp["mix_norm_g"])[0].reshape(8, 128).T
    sF[:, SF_GFFN:SF_GFFN + 8] = f(inp["ffn_norm_g"])[0].reshape(8, 128).T
    sF[:, SF_GA:SF_GA + 4] = f(inp["attn_out_g"])[0].reshape(4, 128).T
    sF[:, SF_GH:SF_GH + 4] = f(inp["hgrn_out_g"])[0].reshape(4, 128).T
    cw = f(inp["conv_w"])[0]
    for k in range(3):
        sF[:, SF_CW + k * NFF:SF_CW + (k + 1) * NFF] = cw[k].reshape(NFF, 128).T
    sF[:, SF_CB:SF_CB + NFF] = f(inp["conv_b"])[0].reshape(NFF, 128).T
    lbl = f(inp["hgrn_lb_logits"])
    sF[:, SF_L0:SF_L0 + 4] = lbl[0].reshape(4, 128).T
    sF[:, SF_L1:SF_L1 + 4] = lbl[1].reshape(4, 128).T
    rowF = np.zeros(NROW, np.float32)
    rowF[RF_FG:RF_FG + D] = f(inp["final_norm_g"])
    rowF[RF_BIH:RF_BIH + 512] = b_in[1792:2304]
    rowF[RF_BVA:RF_BVA + 128] = b_in[640:768]
    rowF[RF_SINK:RF_SINK + 8] = f(inp["attn_sinks"])[0]
    x = f(inp["x"])
    c = f(inp["c"])
    maps = []
    for i in range(n_cores):
        xs = x[i * NSEQ:(i + 1) * NSEQ].reshape(NSEQ * T, D)
        cs = c[i * NSEQ:(i + 1) * NSEQ]
        cTm = np.ascontiguousarray(cs.reshape(NSEQ, 8, 128).transpose(2, 1, 0).reshape(128, 8 * NSEQ))
        maps.append({"xin": np.ascontiguousarray(xs), "wun": wun, "adaun": adaun, "adab": adab, "cT": cTm,
                     "smallF": sF, "rowF": rowF})
    return maps


_NC_CACHE = {}


def kernel(**inputs):
    x = np.asarray(inputs["x"])
    B, T, _ = x.shape
    n_cores = 8 if B % 8 == 0 else 1
    NSEQ = B // n_cores
    key = (T, NSEQ)
    if key not in _NC_CACHE:
        _NC_CACHE[key] = build_nc(T, NSEQ)
    nc = _NC_CACHE[key]
    maps = _host_prep(inputs, T, NSEQ, n_cores)
    res = run_bass_kernel_spmd(nc, maps, core_ids=list(range(n_cores)))
    out = np.stack([np.asarray(r["y"]).reshape(NSEQ, T, D) for r in res.results], axis=0)
    return out.reshape(B, T, D).astype(np.float32)
```

```python
import numpy as np
from contextlib import ExitStack
import concourse.bass as bass
import concourse.mybir as mybir
from concourse.alu_op_type import AluOpType as ALU
from concourse.bass_utils import run_bass_kernel_spmd

F32 = mybir.dt.float32
BF16 = mybir.dt.bfloat16
AF = mybir.ActivationFunctionType
AX = mybir.AxisListType

ENGS = ("pe", "act", "dve", "pool", "sp")
D = 1024
DFF = 2816
NFF = 22
EPS = 1e-6
NUNITS = 96
RS = 8
NEG = -16384.0
_INPROJ_ORDER = [0, 4, 8, 12, 1, 5, 9, 13, 2, 6, 10, 14, 3, 7, 11, 15, 16, 17, 18, 19, 20, 21]
UORDER = _INPROJ_ORDER + list(range(22, NUNITS))
UPOS = {u: i for i, u in enumerate(UORDER)}

SF_BFM = 0
SF_GMIX = 17
SF_GFFN = 25
SF_GA = 33
SF_GH = 37
SF_CW = 41
SF_CB = 107
SF_L0 = 129
SF_L1 = 133
NSF = 137
RF_FG = 0
RF_BIH = 1024
RF_BVA = 1536
RF_SINK = 1664
NROW = 1672


class Prog:
    def __init__(self, nc):
        self.nc = nc
        self.ops = []
        self.last_w = {}
        self.readers = {}
        self.dma_readers = {}
        self.dma_last = {}
        self.eng_last = {e: None for e in ENGS}
        self.epoch = 0

    def op(self, eng, name, *args, reads=(), writes=(), dma=None, extra_deps=(), **kw):
        fn = None if name is None else (name, args, kw)
        idx = len(self.ops)
        deps = set(extra_deps)
        for r in reads:
            t = self.last_w.get(r)
            if t is not None:
                deps.add(t)
        for w in writes:
            t = self.last_w.get(w)
            if t is not None:
                deps.add(t)
            deps.update(self.readers.get(w, {}).values())
            deps.update(self.dma_readers.get(w, ()))
        if dma is not None:
            t = self.dma_last.get(dma)
            if t is not None:
                deps.add(t)
            self.dma_last[dma] = idx
        deps.discard(idx)
        self.ops.append(dict(eng=eng, fn=fn, deps=deps, dma=dma, epoch=self.epoch))
        for r in reads:
            if dma is not None:
                self.dma_readers.setdefault(r, []).append(idx)
            else:
                self.readers.setdefault(r, {})[eng] = idx
        for w in writes:
            self.last_w[w] = idx
            self.readers[w] = {}
            self.dma_readers[w] = []
        self.eng_last[eng] = idx
        return idx

    def barrier(self, engs=ENGS, skip_dma=()):
        deps = set(i for i in self.eng_last.values() if i is not None)
        deps |= set(v for k, v in self.dma_last.items() if not str(k).startswith(skip_dma or "\0"))
        for e in engs:
            self.op(e, None, extra_deps=deps)

    def emit(self, get_sem, dma_sems):
        nc = self.nc
        engobj = dict(pe=nc.tensor, act=nc.scalar, dve=nc.vector, pool=nc.gpsimd, sp=nc.sync)
        ops = self.ops

        def resolve(o):
            out = []
            stack = list(o["deps"])
            visited = set()
            while stack:
                d = stack.pop()
                if d in visited:
                    continue
                visited.add(d)
                od = ops[d]
                if od["fn"] is None:
                    stack.extend(od["deps"])
                    continue
                if od["dma"] is None and od["eng"] == "pe" and o["eng"] == "pe" and o["dma"] is None:
                    continue
                out.append(d)
            return out

        needed = set()
        for o in ops:
            o["rdeps"] = resolve(o)
            for d in o["rdeps"]:
                if ops[d]["dma"] is None:
                    needed.add(d)
        cnt = {}
        dcnt = {}
        for i, o in enumerate(ops):
            if o["fn"] is None:
                continue
            if o["dma"] is not None:
                dcnt[o["dma"]] = dcnt.get(o["dma"], 0) + 16
                o["tok"] = (("dma", o["dma"]), dcnt[o["dma"]])
            elif i in needed:
                k = ("eng", o["eng"], o["epoch"])
                cnt[k] = cnt.get(k, 0) + 1
                o["tok"] = (k, cnt[k])
        self.maxcnt = (max(cnt.values()) if cnt else 0, max(dcnt.values()) if dcnt else 0)
        seen = {e: {} for e in ENGS}
        nwaits = 0
        for i, o in enumerate(ops):
            E = o["eng"]
            eo = engobj[E]
            waits = {}
            for d in o["rdeps"]:
                key, val = ops[d]["tok"]
                if waits.get(key, 0) < val:
                    waits[key] = val
            for key, val in waits.items():
                if seen[E].get(key, 0) < val:
                    sem = dma_sems[key[1]] if key[0] == "dma" else get_sem(key[1], key[2])
                    eo.wait_ge(sem, val)
                    seen[E][key] = val
                    nwaits += 1
            if o["fn"] is None:
                continue
            nm, ar, kw = o["fn"]
            ins = getattr(eo, nm)(*ar, **kw)
            if o["dma"] is not None:
                ins.then_inc(dma_sems[o["dma"]], 16)
            elif i in needed:
                ins.then_inc(get_sem(E, o["epoch"]), 1)
        return nwaits


class Arena:
    def __init__(self, ap):
        self.ap = ap
        self.off = 0

    def take(self, shape, dt):
        n = int(np.prod(shape))
        esz = 4 if dt == F32 else 2
        nw = (n * esz + 3) // 4
        sl = self.ap[:, self.off:self.off + nw]
        self.off += nw
        assert self.off <= self.ap.shape[1], ("arena overflow", self.off, self.ap.shape)
        v = sl if dt == F32 else sl.bitcast(dt)
        if len(shape) == 1:
            return v
        if len(shape) == 2:
            return v.rearrange("p (a b) -> p a b", a=shape[0], b=shape[1])
        if len(shape) == 3:
            return v.rearrange("p (a b c) -> p a b c", a=shape[0], b=shape[1], c=shape[2])
        raise ValueError(shape)


def build_nc(T, NSEQ, dbg=False):
    NT = T // 512
    NTT = NSEQ * NT
    nc = bass.Bass("TRN2", target_bir_lowering=False)
    xin = nc.dram_tensor("xin", [NSEQ * T, D], F32, kind="ExternalInput").ap()
    wun = nc.dram_tensor("wun", [NUNITS * 128, 1024], F32, kind="ExternalInput").ap()
    adaun = nc.dram_tensor("adaun", [12, 128, 8, 512], F32, kind="ExternalInput").ap()
    adab = nc.dram_tensor("adab", [6144], F32, kind="ExternalInput").ap()
    cT = nc.dram_tensor("cT", [128, 8 * NSEQ], F32, kind="ExternalInput").ap()
    smallF_d = nc.dram_tensor("smallF", [128, NSF], F32, kind="ExternalInput").ap()
    rowF_d = nc.dram_tensor("rowF", [NROW], F32, kind="ExternalInput").ap()
    yout_d = nc.dram_tensor("y", [NSEQ * T, D], F32, kind="ExternalOutput").ap()
    wbf = nc.dram_tensor("wbf", [NUNITS * 128, 1024], BF16, kind="Internal").ap()

    with ExitStack() as es:
        def sb(name, shape, dt):
            return es.enter_context(nc.sbuf_tensor(name, shape, dt))[:]

        NEP = (NTT + 1) // 2 + 1
        sems = {(e, ep): es.enter_context(nc.semaphore("s_%s%d" % (e, ep)))
                for e in ("pe", "act", "dve", "pool") for ep in range(NEP)}
        NRG = (NTT + 3) // 4
        dma_keys = (["xl0", "xl1", "yo0", "yo1", "c0", "c1", "c2", "c2b", "c3", "ad0", "ad1"]
                    + ["rg%d_%d" % (i, q) for i in range(RS) for q in range(NRG)] + ["cast%d" % i for i in range(12)])
        dsem = {k: es.enter_context(nc.semaphore("d_" + k)) for k in dma_keys}
        P = Prog(nc)

        xbuf = sb("xbuf", [128, 8, D], F32)
        xn = sb("xn", [128, 4, D], BF16)
        hT = sb("hT", [128, 8, 512], BF16)
        yb = sb("yb", [128, 2, D], F32)
        ring = sb("ring", [128, RS, 1024], BF16)
        KaT = sb("KaT", [128, 2, 2, 512], BF16)
        Vaug = sb("Vaug", [128, 2, 4, 2, 65], BF16)
        Sf = sb("Sf", [128, 4, 128], F32)
        Sbf = sb("Sbf", [128, 2, 4, 128], BF16)
        uprev = sb("uprev", [128, NFF, 2], F32)
        Qe = sb("Qe", [128, 4, 512], BF16)
        Qo = sb("Qo", [128, 4, 512], BF16)
        Khe = sb("Khe", [128, 4, 4, 128], BF16)
        Kho = sb("Kho", [128, 4, 4, 128], BF16)
        ident = sb("ident", [128, 128], BF16)
        identf = sb("identf", [128, 128], F32)
        maskcur = sb("maskcur", [128, 4, 128], BF16)
        maskprev = sb("maskprev", [128, 4, 128], BF16)
        hmask = sb("hmask", [128, 4, 128], BF16)
        scanmask = sb("scanmask", [128, 512], F32)
        neghalf = sb("neghalf", [128, 8], F32)
        smallF = sb("smallFs", [128, NSF], F32)
        rowbc = sb("rowbc", [128, NROW], F32)
        gates = sb("gates", [128, NSEQ, 2, D], F32)
        modT = sb("modT", [128, 4, NSEQ, 8], F32)
        am = sb("am", [128, NSEQ, 8], F32)
        af = sb("af", [128, NSEQ, 8], F32)
        hc = sb("hc", [128, 8, 4], F32)
        expsink = sb("expsink", [128, 8], F32)
        el = sb("el", [128, 4, 8], F32)
        stat = sb("stat", [128, 2, 32], F32)
        arena_t = sb("arena", [128, 19968], F32)

        psb = [es.enter_context(nc.psum_tensor("ps%d" % i, [128, 512], F32))[:] for i in range(8)]
        bank_ctr = [0]

        def bank():
            k = bank_ctr[0] % 8
            bank_ctr[0] += 1
            return k

        MA = Arena(arena_t)
        PT = MA.take([2, 2, 512], BF16)
        attn = MA.take([2, 512], F32)
        attn_n = MA.take([4, 512], BF16)
        rec_n = MA.take([4, 512], BF16)
        Vh = MA.take([4, 512], BF16)
        ST = MA.take([2, 4, 128], BF16)
        osq = MA.take([512], F32)
        osq2 = MA.take([512], F32)
        mixT = MA.take([8, 512], BF16)
        otmp = MA.take([2, 512], F32)
        late_top = MA.off
        QaT = MA.take([4, 512], BF16)
        tA2 = MA.take([2, 512], F32)
        tB2 = MA.take([2, 512], F32)
        tC2 = MA.take([2, 512], F32)
        tD2 = MA.take([2, 512], F32)
        tE2 = MA.take([2, 512], F32)
        silug = MA.take([4, 512], F32)
        KtT = MA.take([4, 512], BF16)
        KhatT = MA.take([4, 512], BF16)
        mixer_top = MA.off
        FA = Arena(arena_t)
        gT = FA.take([NFF, 512], BF16)
        ubuf = FA.take([2, 516], F32)
        sgb = FA.take([2, 512], F32)
        dtmp = FA.take([2, 512], F32)
        fjunk = FA.take([1024], BF16)
        assert FA.off <= late_top, (FA.off, late_top)
        PA = Arena(arena_t)
        adab_bc = PA.take([2, 512], F32)
        adaf = PA.take([2, 8, 512], F32)
        adaw = PA.take([2, 8, 512], BF16)
        mtmp = PA.take([2, 512], F32)
        mtmp2 = PA.take([2, 512], F32)
        maskf = PA.take([4, 128], F32)
        cTs = PA.take([8 * NSEQ], F32)
        cond = PA.take([8 * NSEQ], F32)
        condrep = PA.take([8 * NSEQ, 128], BF16)

        A = P.op

        A("pool", "dma_start", out=smallF, in_=smallF_d, writes=["smallF"], dma="c0")
        A("pool", "dma_start", out=rowbc, in_=rowF_d.partition_broadcast(128), writes=["rowbc"], dma="c1")
        A("pool", "dma_start", out=cTs, in_=cT, writes=["cTs"], dma="c2")
        def cast(k):
            A("pool", "dma_start", out=wbf[k * 1024:(k + 1) * 1024, :], in_=wun[k * 1024:(k + 1) * 1024, :],
              reads=(["adaf0", "adaf1"] if k >= 2 else []), writes=["wbf%d" % k], dma="cast%d" % k)
        def xload(g):
            base = (g % 2) * 4
            r0 = g * 512
            A("pool", "dma_start", out=xbuf[:, base:base + 4, :],
                                            in_=xin[r0:r0 + 512, :].rearrange("(s p) d -> p s d", p=128),
              writes=["x%d" % (base + s) for s in range(4)], dma="xl%d" % (g % 2))
        xload(0)
        cast(0)
        cast(1)

        A("pool", "memset", identf, 0.0, writes=["identf"])
        A("pool", "affine_select", out=identf, in_=identf, pattern=[[-1, 128]], compare_op=ALU.not_equal,
                                            fill=1.0, base=0, channel_multiplier=1, reads=["identf"], writes=["identf"])
        A("dve", "tensor_copy", out=ident, in_=identf, reads=["identf"], writes=["ident"])
        A("pool", "memset", maskf, 0.0, writes=["maskf"])
        A("pool", "affine_select", out=maskf, in_=maskf, pattern=[[0, 4], [1, 128]], compare_op=ALU.is_ge,
                                            fill=NEG, base=0, channel_multiplier=-1, reads=["maskf"], writes=["maskf"])
        A("dve", "tensor_copy", out=maskcur, in_=maskf, reads=["maskf"], writes=["maskcur"])
        A("pool", "memset", maskf, 0.0, reads=["maskf"], writes=["maskf"])
        A("pool", "affine_select", out=maskf, in_=maskf, pattern=[[0, 4], [-1, 128]], compare_op=ALU.is_gt,
                                            fill=NEG, base=0, channel_multiplier=1, reads=["maskf"], writes=["maskf"])
        A("dve", "tensor_copy", out=maskprev, in_=maskf, reads=["maskf"], writes=["maskprev"])
        A("pool", "memset", maskf, 1.0, reads=["maskf"], writes=["maskf"])
        A("pool", "affine_select", out=maskf, in_=maskf, pattern=[[0, 4], [1, 128]], compare_op=ALU.is_ge,
                                            fill=0.0, base=0, channel_multiplier=-1, reads=["maskf"], writes=["maskf"])
        A("pool", "memset", maskf[0:64, :, 64:128], 0.0, reads=["maskf"], writes=["maskf"])
        A("dve", "tensor_copy", out=hmask, in_=maskf, reads=["maskf"], writes=["hmask"])
        A("pool", "memset", scanmask, 0.0, writes=["scanmask"])
        A("pool", "memset", scanmask.rearrange("p (c t) -> p c t", t=64)[:, :, 0:1], 1.0,
          reads=["scanmask"], writes=["scanmask"])
        A("pool", "memset", neghalf, -0.5, writes=["neghalf"])
        A("pool", "memset", KaT, 0.0, writes=["KaT0_0", "KaT0_1", "KaT1_0", "KaT1_1"])
        A("pool", "memset", Vaug, 1.0, writes=["Vaug0", "Vaug1"])
        A("pool", "memset", Qe, 0.0, writes=["Qe%d" % c for c in range(4)])
        A("pool", "memset", Qo, 0.0, writes=["Qo%d" % c for c in range(4)])
        A("pool", "memset", Khe, 0.0, writes=["Khe%d" % c for c in range(4)])
        A("pool", "memset", Kho, 0.0, writes=["Kho%d" % c for c in range(4)])

        lb, oml, homl, nhoml, lbh, halfbf = (hc[:, i, :] for i in range(6))
        A("dve", "tensor_tensor", out=hc[:, 6, :], in0=smallF[:, SF_L0:SF_L0 + 4], in1=smallF[:, SF_L1:SF_L1 + 4],
                                           op=ALU.subtract, reads=["smallF"], writes=["hc6"])
        A("act", "activation", out=lb, in_=hc[:, 6, :], func=AF.Sigmoid, reads=["hc6"], writes=["lb"])
        A("dve", "tensor_scalar", out=oml, in0=lb, scalar1=-1.0, scalar2=1.0, op0=ALU.mult, op1=ALU.add,
          reads=["lb"], writes=["oml"])
        A("dve", "tensor_scalar", out=homl, in0=oml, scalar1=0.5, scalar2=None, op0=ALU.mult,
          reads=["oml"], writes=["homl"])
        A("dve", "tensor_scalar", out=nhoml, in0=oml, scalar1=-0.5, scalar2=None, op0=ALU.mult,
          reads=["oml"], writes=["nhoml"])
        A("dve", "tensor_tensor", out=lbh, in0=lb, in1=homl, op=ALU.add, reads=["lb", "homl"], writes=["lbh"])
        A("dve", "tensor_scalar", out=halfbf, in0=smallF[:, SF_BFM:SF_BFM + 4], scalar1=0.5, scalar2=None,
                                           op0=ALU.mult, reads=["smallF"], writes=["halfbf"])
        HC = ["lb", "oml", "homl", "nhoml", "lbh", "halfbf"]
        A("act", "activation", out=expsink, in_=rowbc[:, RF_SINK:RF_SINK + 8], func=AF.Exp,
          reads=["rowbc"], writes=["expsink"])

        A("act", "activation", out=cond, in_=cTs, func=AF.Silu, reads=["cTs"], writes=["cond"])
        A("dve", "tensor_copy", out=condrep, in_=cond.unsqueeze(2).to_broadcast([128, 8 * NSEQ, 128]),
          reads=["cond"], writes=["condrep"])
        FIELD = {0: 0, 1: 1, 3: 2, 4: 3}
        for cb in range(12):
            ab = cb % 2
            A("sp", "dma_start", out=adaf[:, ab, :, :], in_=adaun[cb], writes=["adaf%d" % ab], dma="ad%d" % ab)
            A("pool", "dma_start", out=adab_bc[:, ab, :], in_=adab[cb * 512:(cb + 1) * 512].partition_broadcast(128),
              writes=["adab%d" % ab], dma="c3" if ab else "c2b")
            A("act", "activation", out=adaw[:, ab, 0:4, :], in_=adaf[:, ab, 0:4, :], func=AF.Identity,
              reads=["adaf%d" % ab], writes=["adaw%d" % ab])
            A("dve", "tensor_copy", out=adaw[:, ab, 4:8, :], in_=adaf[:, ab, 4:8, :],
              reads=["adaf%d" % ab], writes=["adaw%d_b" % ab])
            field, half = cb // 2, cb % 2
            for b in range(NSEQ):
                bk = bank()
                for kc in range(8):
                    A("pe", "matmul",
                        psb[bk], lhsT=condrep[:, kc * NSEQ + b, :], rhs=adaw[:, ab, kc, :],
                        start=(kc == 0), stop=(kc == 7),
                      reads=["condrep", "adaw%d" % ab, "adaw%d_b" % ab], writes=["ps%d" % bk])
                if field in (2, 5):
                    gi = 0 if field == 2 else 1
                    A("dve", "tensor_tensor",
                        out=gates[:, b, gi, half * 512:(half + 1) * 512], in0=psb[bk], in1=adab_bc[:, ab, :], op=ALU.add,
                      reads=["ps%d" % bk, "adab%d" % ab], writes=["gates"])
                else:
                    fi = FIELD[field]
                    mb = b
                    A("dve", "tensor_tensor",
                        out=mtmp[:, mb, :], in0=psb[bk], in1=adab_bc[:, ab, :], op=ALU.add,
                      reads=["ps%d" % bk, "adab%d" % ab], writes=["mtmp%d" % mb])
                    A("dve", "tensor_tensor",
                        out=mtmp2[:, mb, :].rearrange("p (a b) -> p a b", a=4),
                        in0=mtmp[:, mb, :].rearrange("p (a b) -> p a b", a=4),
                        in1=identf.unsqueeze(1).to_broadcast([128, 4, 128]), op=ALU.mult,
                      reads=["mtmp%d" % mb, "identf"], writes=["mtmp2%d" % mb])
                    A("dve", "tensor_reduce",
                        out=modT[:, fi, b, half * 4:(half + 1) * 4],
                        in_=mtmp2[:, mb, :].rearrange("p (a b) -> p a b", a=4), axis=AX.X, op=ALU.add,
                      reads=["mtmp2%d" % mb], writes=["modT"])
        cast(2)
        cast(3)
        for b in range(NSEQ):
            A("dve", "scalar_tensor_tensor", out=am[:, b, :], in0=modT[:, 1, b, :], scalar=1.0,
                                                           in1=smallF[:, SF_GMIX:SF_GMIX + 8], op0=ALU.add, op1=ALU.mult,
              reads=["modT", "smallF"], writes=["am"])
            A("dve", "scalar_tensor_tensor", out=af[:, b, :], in0=modT[:, 3, b, :], scalar=1.0,
                                                           in1=smallF[:, SF_GFFN:SF_GFFN + 8], op0=ALU.add, op1=ALU.mult,
              reads=["modT", "smallF"], writes=["af"])
        P.barrier(skip_dma=("cast",))

        unit_ctr = [0]
        cur_g = [0]

        def load_unit(uidx):
            slot = unit_ctr[0] % RS
            unit_ctr[0] += 1
            upos = UPOS[uidx]
            A("sp", "dma_start", out=ring[:, slot, :], in_=wbf[upos * 128:(upos + 1) * 128, :],
              reads=["wbf%d" % (upos // 8)], writes=["rg%d" % slot], dma="rg%d_%d" % (slot, cur_g[0] // 4))
            return slot

        def norm_stats(g, tagp):
            par = g % 2
            ss = stat[:, par, 0:4] if tagp == 0 else stat[:, par, 8:12]
            rs = stat[:, par, 4:8] if tagp == 0 else stat[:, par, 12:16]
            sn = "n%d_%d" % (tagp, par)
            for s in range(4):
                sl = (g % 2) * 4 + s
                A("act", "activation", out=xn[:, s, :], in_=xbuf[:, sl, :], func=AF.Square,
                  accum_out=ss[:, s:s + 1], reads=["x%d" % sl], writes=["xn%d" % s, sn + "ss%d" % s])
            A("pool", "tensor_scalar", out=ss, in0=ss, scalar1=1.0 / D, scalar2=EPS, op0=ALU.mult, op1=ALU.add,
              reads=[sn + "ss%d" % s for s in range(4)], writes=[sn + "ssb"])
            A("pool", "tensor_tensor", out=rs, in0=ss, in1=neghalf[:, 0:4], op=ALU.pow,
              reads=[sn + "ssb", "neghalf"] + [sn + "ss%d" % s for s in range(4)], writes=[sn + "rs"])
            for s in range(4):
                sl = (g % 2) * 4 + s
                if s % 2 == 0:
                    A("dve", "tensor_scalar", out=xn[:, s, :], in0=xbuf[:, sl, :], scalar1=rs[:, s:s + 1],
                      scalar2=None, op0=ALU.mult, reads=["x%d" % sl, sn + "rs"], writes=["xn%d" % s])
                else:
                    A("act", "activation", out=xn[:, s, :], in_=xbuf[:, sl, :], func=AF.Identity, scale=rs[:, s:s + 1],
                      reads=["x%d" % sl, sn + "rs"], writes=["xn%d" % s])

        def norm_trans(b, a_t, sh_ap):
            for c in range(8):
                bk = bank()
                pb = psb[bk].bitcast(BF16)
                for s in range(4):
                    A("pe", "transpose", out=pb[:, s * 128:(s + 1) * 128], in_=xn[:, s, c * 128:(c + 1) * 128],
                      identity=ident, reads=["xn%d" % s, "ident"], writes=["ps%d" % bk])
                if c % 2 == 0:
                    A("act", "activation", out=hT[:, c, :], in_=pb[:, 0:512], func=AF.Identity,
                      scale=a_t[:, b, c:c + 1], bias=sh_ap[:, b, c:c + 1],
                      reads=["ps%d" % bk, "am", "af", "modT"], writes=["hT%d" % c])
                else:
                    A("dve", "tensor_scalar", out=hT[:, c, :], in0=pb[:, 0:512], scalar1=a_t[:, b, c:c + 1],
                      scalar2=sh_ap[:, b, c:c + 1], op0=ALU.mult, op1=ALU.add,
                      reads=["ps%d" % bk, "am", "af", "modT"], writes=["hT%d" % c])

        HTALL = ["hT%d" % c for c in range(8)]

        def fm_chunk(uidx):
            slot = load_unit(uidx)
            bk = bank()
            for kc in range(8):
                A("pe", "matmul", psb[bk], lhsT=ring[:, slot, kc * 128:(kc + 1) * 128], rhs=hT[:, kc, :],
                  start=(kc == 0), stop=(kc == 7), reads=["rg%d" % slot] + HTALL, writes=["ps%d" % bk])
            return bk

        def bias(f):
            return smallF[:, SF_BFM + f:SF_BFM + f + 1]

        ARENA_ENGS = ("act", "dve", "pool")

        norm_stats(0, 0)
        norm_trans(0, am, modT[:, 0, :, :])
        for g in range(NTT):
            b = g // NT
            ti = g % NT
            par = g % 2
            cur_g[0] = g
            P.epoch = 1 + g // 2
            if g + 1 < NTT:
                xload(g + 1)
            if ti == 0:
                A("pool", "memset", Sf, 0.0, writes=["Sf0", "Sf1", "Sf2", "Sf3"])
                A("pool", "memset", Sbf[:, 0, :, :], 0.0, writes=["Sbf0"])
                A("pool", "memset", uprev, 0.0, writes=["uprev%d" % j for j in range(NFF)])

            def hg_pre(c):
                hp = c % 2
                tA, tB, tC, tD, tE = tA2[:, hp, :], tB2[:, hp, :], tC2[:, hp, :], tD2[:, hp, :], tE2[:, hp, :]
                bk = fm_chunk(c)
                A("act", "activation", out=tA, in_=psb[bk], func=AF.Tanh, scale=0.5, bias=halfbf[:, c:c + 1],
                  reads=["ps%d" % bk] + HC, writes=["tA%d" % hp])
                bk2 = fm_chunk(4 + c)
                A("act", "activation", out=tE, in_=psb[bk2], func=AF.Silu, bias=bias(4 + c),
                  reads=["ps%d" % bk2, "smallF"], writes=["tE%d" % hp])
                A("act", "activation", out=tB, in_=tA, func=AF.Identity, scale=nhoml[:, c:c + 1], bias=homl[:, c:c + 1],
                  reads=["tA%d" % hp] + HC, writes=["tB%d" % hp])
                A("act", "activation", out=tC, in_=tA, func=AF.Identity, scale=homl[:, c:c + 1], bias=lbh[:, c:c + 1],
                  reads=["tA%d" % hp] + HC, writes=["tC%d" % hp])
                A("dve", "tensor_tensor_scan", out=tD, data0=scanmask, data1=tC, initial=0.0, op0=ALU.max,
                  op1=ALU.mult, reads=["tC%d" % hp, "scanmask"], writes=["tD%d" % hp])
                A("dve", "reciprocal", out=tC, in_=tD, reads=["tD%d" % hp], writes=["tC%d" % hp])
                tE4 = tE.rearrange("p (j two t) -> p j two t", two=2, t=64)
                tD4 = tD.rearrange("p (j two t) -> p j two t", two=2, t=64)
                Qe4 = Qe[:, c, :].rearrange("p (j two t) -> p j two t", two=2, t=64)
                Qo4 = Qo[:, c, :].rearrange("p (j two t) -> p j two t", two=2, t=64)
                A("dve", "tensor_tensor", out=tB, in0=tB, in1=tC, op=ALU.mult, reads=["tB%d" % hp, "tC%d" % hp], writes=["tB%d" % hp])
                tB3 = tB.rearrange("p (j t) -> p j t", t=64)
                tD3 = tD.rearrange("p (j t) -> p j t", t=64)
                A("dve", "tensor_tensor", out=KhatT[:, c, :].rearrange("p (j t) -> p j t", t=64), in0=tB3,
                  in1=tD3[:, :, 63:64].to_broadcast([128, 8, 64]), op=ALU.mult,
                  reads=["tB%d" % hp, "tD%d" % hp], writes=["KhatT%d" % c])
                A("dve", "tensor_tensor", out=Qe4[:, :, 0, :], in0=tE4[:, :, 0, :], in1=tD4[:, :, 0, :], op=ALU.mult,
                  reads=["tE%d" % hp, "tD%d" % hp], writes=["Qe%d" % c])
                A("dve", "tensor_tensor", out=Qo4[:, :, 1, :], in0=tE4[:, :, 1, :], in1=tD4[:, :, 1, :], op=ALU.mult,
                  reads=["tE%d" % hp, "tD%d" % hp], writes=["Qo%d" % c])
                A("pool", "tensor_copy", out=KtT[:, c, :], in_=tB, reads=["tB%d" % hp], writes=["KtT%d" % c])
                A("pool", "tensor_copy", out=el[:, c, :], in_=tD3[:, :, 63], reads=["tD%d" % hp], writes=["el%d" % c])

            def hg_trans(c):
                bk3 = bank()
                pb = psb[bk3].bitcast(BF16)
                for s in range(4):
                    A("pe", "transpose", out=pb[:, s * 128:(s + 1) * 128], in_=KhatT[:, c, s * 128:(s + 1) * 128],
                      identity=ident, reads=["KhatT%d" % c, "ident"], writes=["ps%d" % bk3])
                pb3 = pb[:, 0:512].rearrange("p (s k) -> p s k", s=4)
                A("act", "activation", out=Khe[0:64, :, c, :], in_=pb3[0:64], func=AF.Identity,
                  reads=["ps%d" % bk3], writes=["Khe%d" % c])
                A("act", "activation", out=Kho[64:128, :, c, :], in_=pb3[64:128], func=AF.Identity,
                  reads=["ps%d" % bk3], writes=["Kho%d" % c])

            def g_chunk(c):
                bk = fm_chunk(8 + c)
                A("act", "activation", out=silug[:, c, :], in_=psb[bk], func=AF.Silu, bias=bias(8 + c),
                  reads=["ps%d" % bk, "smallF"], writes=["silug%d" % c])

            def ka_chunk():
                bk = fm_chunk(12)
                A("act", "activation", out=KaT[0:64, par, 0, :], in_=psb[bk][0:64], func=AF.Identity,
                  bias=bias(12)[0:64], reads=["ps%d" % bk, "smallF"], writes=["KaT%d_0" % par])
                A("act", "activation", out=KaT[64:128, par, 1, :], in_=psb[bk][64:128], func=AF.Identity,
                  bias=bias(12)[64:128], reads=["ps%d" % bk, "smallF"], writes=["KaT%d_1" % par])

            def qa_chunk(gq):
                bk = fm_chunk(13 + gq)
                A("act", "activation", out=QaT[:, gq, :], in_=psb[bk], func=AF.Identity, bias=bias(13 + gq),
                  reads=["ps%d" % bk, "smallF"], writes=["QaT%d" % gq])

            def ih_group():
                bks = [bank() for _ in range(4)]
                for u in range(4):
                    slot = load_unit(17 + u)
                    for kk in range(2):
                        kc = 2 * u + kk
                        for s in range(4):
                            A("pe", "matmul", psb[bks[s]], lhsT=hT[:, kc, s * 128:(s + 1) * 128],
                              rhs=ring[:, slot, kk * 512:(kk + 1) * 512], start=(kc == 0), stop=(kc == 7),
                              reads=["rg%d" % slot, "hT%d" % kc], writes=["ps%d" % bks[s]])
                for s in range(4):
                    A("dve", "tensor_tensor", out=Vh[:, s, :], in0=psb[bks[s]], in1=rowbc[:, RF_BIH:RF_BIH + 512],
                      op=ALU.add, reads=["ps%d" % bks[s], "rowbc"], writes=["Vh%d" % s])

            def va_group():
                slot = load_unit(21)
                bk = bank()
                for s in range(4):
                    for kc in range(8):
                        A("pe", "matmul", psb[bk][:, s * 128:(s + 1) * 128], lhsT=hT[:, kc, s * 128:(s + 1) * 128],
                          rhs=ring[:, slot, kc * 128:(kc + 1) * 128], start=(kc == 0), stop=(kc == 7),
                          reads=["rg%d" % slot, "hT%d" % kc], writes=["ps%d" % bk])
                A("dve", "tensor_tensor", out=Vaug[:, par, :, :, 0:64],
                  in0=psb[bk].rearrange("p (s h d) -> p s h d", s=4, h=2),
                  in1=rowbc[:, RF_BVA:RF_BVA + 128].rearrange("p (h d) -> p h d", h=2).unsqueeze(1).to_broadcast([128, 4, 2, 64]),
                  op=ALU.add, reads=["ps%d" % bk, "rowbc"], writes=["Vaug%d" % par])

            hg_pre(0); g_chunk(0); ka_chunk()
            if g == 0:
                cast(4)
            hg_pre(1); g_chunk(1); qa_chunk(0)
            hg_pre(2); g_chunk(2); qa_chunk(1); hg_trans(0)
            if g == 0:
                cast(5)
            hg_pre(3); g_chunk(3); qa_chunk(2); hg_trans(1)
            qa_chunk(3)
            if g > 0:
                P.barrier(ARENA_ENGS, skip_dma=("cast",))
            ih_group(); hg_trans(2); va_group(); hg_trans(3)
            if g == 0:
                cast(6)
                cast(7)

            KTT = ["KtT%d" % c for c in range(4)]
            QE = ["Qe%d" % c for c in range(4)]
            QO = ["Qo%d" % c for c in range(4)]
            KHE = ["Khe%d" % c for c in range(4)]
            KHO = ["Kho%d" % c for c in range(4)]
            EL = ["el%d" % c for c in range(4)]
            QAT = ["QaT%d" % c for c in range(4)]

            pst = stat[:, par, :]
            pending_tail = []
            for s in range(4):
                first = (ti == 0 and s == 0)
                cols = slice(s * 128, (s + 1) * 128)
                sb_ = s % 2
                bsc = bank()
                for c in range(4):
                    A("pe", "matmul", psb[bsc][:, c * 128:(c + 1) * 128], lhsT=KtT[:, c, cols], rhs=Qe[:, c, cols],
                      start=True, stop=False, reads=["KtT%d" % c, "Qe%d" % c], writes=["ps%d" % bsc])
                    A("pe", "matmul", psb[bsc][:, c * 128:(c + 1) * 128], lhsT=KtT[:, c, cols], rhs=Qo[:, c, cols],
                      start=False, stop=True, reads=["KtT%d" % c, "Qo%d" % c], writes=["ps%d" % bsc])

                def kv_mm(khat, khnames):
                    bkv = bank()
                    for c in range(4):
                        A("pe", "matmul", psb[bkv][:, c * 128:(c + 1) * 128], lhsT=khat[:, s, c, :],
                          rhs=Vh[:, s, c * 128:(c + 1) * 128], start=True, stop=True,
                          reads=[khnames[c], "Vh%d" % s], writes=["ps%d" % bkv])
                    return bkv

                def state_upd(bkv, j, dst):
                    for c in range(4):
                        A("dve", "scalar_tensor_tensor", out=Sf[:, c, :], in0=Sf[:, c, :], scalar=el[:, c, j:j + 1],
                          in1=psb[bkv][:, c * 128:(c + 1) * 128], op0=ALU.mult, op1=ALU.add,
                          reads=["Sf%d" % c, "el%d" % c, "ps%d" % bkv], writes=["Sf%d" % c])
                    A("act", "activation", out=Sbf[:, dst, :, :], in_=Sf, func=AF.Identity,
                      reads=["Sf0", "Sf1", "Sf2", "Sf3"], writes=["Sbf%d" % dst])

                bkv_e = kv_mm(Khe, KHE)
                prevs = {}
                for h in range(2):
                    bc_ = bank()
                    A("pe", "matmul", psb[bc_], lhsT=KaT[:, par, h, cols], rhs=QaT[:, :, cols], start=True, stop=False,
                      reads=["KaT%d_%d" % (par, h)] + QAT, writes=["ps%d" % bc_])
                    A("pe", "matmul", psb[bc_], lhsT=ident, rhs=maskcur.rearrange("p a b -> p (a b)"),
                      start=False, stop=True, reads=["ident", "maskcur"], writes=["ps%d" % bc_])
                    A("act", "activation", out=PT[:, h, 1, :], in_=psb[bc_], func=AF.Exp, scale=0.125,
                      reads=["ps%d" % bc_], writes=["PTc%d" % h])
                    if not first:
                        bp_ = bank()
                        if s > 0:
                            kprev = KaT[:, par, h, (s - 1) * 128:s * 128]
                            kname = "KaT%d_%d" % (par, h)
                            vprev = Vaug[:, par, s - 1, h, :]
                            vname = "Vaug%d" % par
                        else:
                            kprev = KaT[:, 1 - par, h, 384:512]
                            kname = "KaT%d_%d" % (1 - par, h)
                            vprev = Vaug[:, 1 - par, 3, h, :]
                            vname = "Vaug%d" % (1 - par)
                        prevs[h] = (vprev, vname)
                        A("pe", "matmul", psb[bp_], lhsT=kprev, rhs=QaT[:, :, cols], start=True, stop=False,
                          reads=[kname] + QAT, writes=["ps%d" % bp_])
                        A("pe", "matmul", psb[bp_], lhsT=ident, rhs=maskprev.rearrange("p a b -> p (a b)"),
                          start=False, stop=True, reads=["ident", "maskprev"], writes=["ps%d" % bp_])
                        A("act", "activation", out=PT[:, h, 0, :], in_=psb[bp_], func=AF.Exp, scale=0.125,
                          reads=["ps%d" % bp_], writes=["PTp%d" % h])
                A("dve", "tensor_tensor", out=ST[:, sb_, :, :], in0=psb[bsc].rearrange("p (c t) -> p c t", c=4),
                  in1=hmask, op=ALU.mult, reads=["ps%d" % bsc, "hmask"], writes=["ST%d" % sb_])
                state_upd(bkv_e, 2 * s, 1)
                while pending_tail:
                    pending_tail.pop(0)()
                if s > 0:
                    bank()
                for h in range(2):
                    bo_ = bank()
                    for gq in range(4):
                        oreg = psb[bo_][:, gq * 65:(gq + 1) * 65]
                        A("pe", "matmul", oreg, lhsT=PT[:, h, 1, gq * 128:(gq + 1) * 128], rhs=Vaug[:, par, s, h, :],
                          start=True, stop=first, reads=["PTc%d" % h, "Vaug%d" % par], writes=["ps%d" % bo_])
                        if not first:
                            A("pe", "matmul", oreg, lhsT=PT[:, h, 0, gq * 128:(gq + 1) * 128], rhs=prevs[h][0],
                              start=False, stop=True, reads=["PTp%d" % h, prevs[h][1]], writes=["ps%d" % bo_])
                    o3 = psb[bo_][:, 0:260].rearrange("p (g d) -> p g d", g=4)
                    den = pst[:, 16 + h * 4:20 + h * 4]
                    dn = "den%d%d" % (par, h)
                    A("dve", "tensor_tensor", out=den, in0=o3[:, :, 64], in1=expsink[:, h * 4:(h + 1) * 4], op=ALU.add,
                      reads=["ps%d" % bo_, "expsink"], writes=[dn])
                    A("dve", "reciprocal", out=den, in_=den, reads=[dn], writes=[dn])
                    A("dve", "tensor_tensor", out=attn[:, sb_, h * 256:(h + 1) * 256].rearrange("p (g d) -> p g d", g=4),
                      in0=o3[:, :, 0:64], in1=den.unsqueeze(2).to_broadcast([128, 4, 64]), op=ALU.mult,
                      reads=["ps%d" % bo_, dn], writes=["attn%d_%d" % (sb_, h)])
                bo2 = bank()
                for c in range(4):
                    oreg = psb[bo2][:, c * 128:(c + 1) * 128]
                    A("pe", "matmul", oreg, lhsT=ST[:, sb_, c, :], rhs=Vh[:, s, c * 128:(c + 1) * 128],
                      start=True, stop=False, reads=["ST%d" % sb_, "Vh%d" % s], writes=["ps%d" % bo2])
                    A("pe", "matmul", oreg, lhsT=Qe[:, c, cols], rhs=Sbf[:, 0, c, :], start=False, stop=False,
                      reads=["Qe%d" % c, "Sbf0"], writes=["ps%d" % bo2])
                    A("pe", "matmul", oreg, lhsT=Qo[:, c, cols], rhs=Sbf[:, 1, c, :], start=False, stop=True,
                      reads=["Qo%d" % c, "Sbf1"], writes=["ps%d" % bo2])
                bkv_o = kv_mm(Kho, KHO)
                ssa = pst[:, 24:25]
                rsa = pst[:, 25:26]
                an = ["attn%d_0" % sb_, "attn%d_1" % sb_]
                A("act", "activation", out=osq, in_=attn[:, sb_, :], func=AF.Square, accum_out=ssa,
                  reads=an, writes=["osq", "ssa%d" % par])
                ssh = pst[:, 26:30]
                A("act", "activation", out=osq2, in_=psb[bo2], func=AF.Square, reads=["ps%d" % bo2], writes=["osq2"])
                state_upd(bkv_o, 2 * s + 1, 0)
                def epi_tail(s=s, sb_=sb_, bo2=bo2, an=an, ssa=ssa, rsa=rsa, ssh=ssh):
                    A("pool", "tensor_scalar", out=ssa, in0=ssa, scalar1=1.0 / 512, scalar2=EPS, op0=ALU.mult, op1=ALU.add,
                      reads=["ssa%d" % par], writes=["ssa%d" % par])
                    A("pool", "tensor_tensor", out=rsa, in0=ssa, in1=neghalf[:, 0:1], op=ALU.pow,
                      reads=["ssa%d" % par, "neghalf"], writes=["rsa%d" % par])
                    A("pool", "tensor_scalar", out=attn_n[:, s, :], in0=attn[:, sb_, :], scalar1=rsa, scalar2=1.0,
                      op0=ALU.mult, op1=ALU.mult, reads=an + ["rsa%d" % par], writes=["attn_n%d" % s])
                    A("dve", "tensor_reduce", out=ssh, in_=osq2.rearrange("p (c v) -> p c v", c=4), axis=AX.X, op=ALU.add,
                      reads=["osq2"], writes=["ssh%d" % par])
                    A("pool", "tensor_scalar", out=ssh, in0=ssh, scalar1=1.0 / 128, scalar2=EPS, op0=ALU.mult, op1=ALU.add,
                      reads=["ssh%d" % par], writes=["ssh%d" % par])
                    A("pool", "tensor_tensor", out=ssh, in0=ssh, in1=neghalf[:, 0:4], op=ALU.pow,
                      reads=["ssh%d" % par, "neghalf"], writes=["ssh%d" % par])
                    A("dve", "tensor_tensor", out=rec_n[:, s, :].rearrange("p (c v) -> p c v", c=4),
                      in0=psb[bo2].rearrange("p (c v) -> p c v", c=4), in1=ssh.unsqueeze(2).to_broadcast([128, 4, 128]),
                      op=ALU.mult, reads=["ps%d" % bo2, "ssh%d" % par], writes=["rec_n%d" % s])
                pending_tail.append(epi_tail)
            while pending_tail:
                pending_tail.pop(0)()

            mbanks = [bank() for _ in range(4)]
            mpb = [psb[bk].bitcast(BF16) for bk in mbanks]
            for sgrp in ((0, 1, 2), (3,)):
                for c in range(8):
                    src = attn_n if c < 4 else rec_n
                    sname = "attn_n" if c < 4 else "rec_n"
                    cc = c % 4
                    off = (c % 2) * 512
                    for s in sgrp:
                        A("pe", "transpose", out=mpb[c // 2][:, off + s * 128:off + (s + 1) * 128],
                          in_=src[:, s, cc * 128:(cc + 1) * 128], identity=ident,
                          reads=[sname + "%d" % s, "ident"], writes=["ps%d" % mbanks[c // 2]])
            for c in range(8):
                cc = c % 4
                bk = mbanks[c // 2]
                pbc = mpb[c // 2][:, (c % 2) * 512:(c % 2) * 512 + 512]
                if c < 4:
                    A("act", "activation", out=mixT[:, c, :], in_=pbc, func=AF.Identity,
                      scale=smallF[:, SF_GA + c:SF_GA + c + 1], reads=["ps%d" % bk, "smallF"], writes=["mixT%d" % c])
                else:
                    A("dve", "scalar_tensor_tensor", out=mixT[:, c, :], in0=pbc,
                      scalar=smallF[:, SF_GH + cc:SF_GH + cc + 1], in1=silug[:, cc, :], op0=ALU.mult, op1=ALU.mult,
                      reads=["ps%d" % bk, "smallF", "silug%d" % cc], writes=["mixT%d" % c])

            def proj_residual(unit0, nK, src, srcname, gi, tmpbuf, tmpname):
                for nh in range(2):
                    bks_ = [bank() for _ in range(4)]
                    for u in range(nK // 2):
                        slot_ = load_unit(unit0 + nh * (nK // 2) + u)
                        for kk in range(2):
                            kc = 2 * u + kk
                            for s in range(4):
                                A("pe", "matmul", psb[bks_[s]], lhsT=src[:, kc, s * 128:(s + 1) * 128],
                                  rhs=ring[:, slot_, kk * 512:(kk + 1) * 512], start=(kc == 0), stop=(kc == nK - 1),
                                  reads=["rg%d" % slot_, srcname + "%d" % kc], writes=["ps%d" % bks_[s]])
                    for s in range(4):
                        sl = (g % 2) * 4 + s
                        tb = s % 2
                        A("dve", "tensor_tensor", out=tmpbuf[:, tb, :], in0=psb[bks_[s]],
                          in1=gates[:, b, gi, nh * 512:(nh + 1) * 512], op=ALU.mult,
                          reads=["ps%d" % bks_[s], "gates"], writes=tmpname(tb))
                        A("dve" if (nh == 1 and s % 2 == 1 and gi == 0) else "pool", "tensor_tensor",
                          out=xbuf[:, sl, nh * 512:(nh + 1) * 512],
                          in0=xbuf[:, sl, nh * 512:(nh + 1) * 512], in1=tmpbuf[:, tb, :], op=ALU.add,
                          reads=["x%d" % sl] + tmpname(tb), writes=["x%d" % sl])

            if g == 0:
                cast(8)
                cast(9)
            proj_residual(22, 8, mixT, "mixT", 0, otmp, lambda tb: ["otmp%d" % tb])

            norm_stats(g, 1)
            norm_trans(b, af, modT[:, 2, :, :])
            P.barrier(ARENA_ENGS)

            def ffn_a(j):
                ub = j % 2
                if g == 0 and j in (0, 6):
                    cast(10 if j == 0 else 11)
                slot_u = load_unit(30 + 2 * j)
                bu = bank()
                for kc in range(8):
                    A("pe", "matmul", psb[bu], lhsT=ring[:, slot_u, kc * 128:(kc + 1) * 128], rhs=hT[:, kc, :],
                      start=(kc == 0), stop=(kc == 7), reads=["rg%d" % slot_u] + HTALL, writes=["ps%d" % bu])
                slot_v = load_unit(31 + 2 * j)
                bv = bank()
                for kc in range(8):
                    A("pe", "matmul", psb[bv], lhsT=ring[:, slot_v, kc * 128:(kc + 1) * 128], rhs=hT[:, kc, :],
                      start=(kc == 0), stop=(kc == 7), reads=["rg%d" % slot_v] + HTALL, writes=["ps%d" % bv])
                un = "ubuf%d" % ub
                A("pool", "tensor_copy", out=ubuf[:, ub, 0:2], in_=uprev[:, j, :], reads=["uprev%d" % j], writes=[un + "h"])
                A("act", "activation", out=ubuf[:, ub, 2:514], in_=psb[bu], func=AF.Identity,
                  reads=["ps%d" % bu], writes=[un])
                A("pool", "tensor_copy", out=uprev[:, j, :], in_=ubuf[:, ub, 512:514], reads=[un], writes=["uprev%d" % j])
                cw = lambda k, j=j: smallF[:, SF_CW + k * NFF + j:SF_CW + k * NFF + j + 1]
                A("act", "activation", out=psb[bu], in_=psb[bu], func=AF.Identity, scale=cw(2),
                  bias=smallF[:, SF_CB + j:SF_CB + j + 1], reads=["ps%d" % bu, "smallF"], writes=["ps%d" % bu])
                A("dve", "scalar_tensor_tensor", out=psb[bu], in0=ubuf[:, ub, 1:513], scalar=cw(1), in1=psb[bu],
                  op0=ALU.mult, op1=ALU.add, reads=[un, un + "h", "ps%d" % bu, "smallF"], writes=["ps%d" % bu])
                A("dve", "scalar_tensor_tensor", out=sgb[:, ub, :], in0=ubuf[:, ub, 0:512], scalar=cw(0), in1=psb[bu],
                  op0=ALU.mult, op1=ALU.add, reads=[un, un + "h", "ps%d" % bu, "smallF"], writes=["sgb%d" % ub])
                return bv

            def ffn_b(j, bv):
                ub = j % 2
                A("act", "activation", out=sgb[:, ub, :], in_=sgb[:, ub, :], func=AF.Silu,
                  reads=["sgb%d" % ub], writes=["sgb%d" % ub])
                A("dve", "tensor_tensor", out=gT[:, j, :], in0=sgb[:, ub, :], in1=psb[bv], op=ALU.mult,
                  reads=["sgb%d" % ub, "ps%d" % bv], writes=["gT%d" % j])

            bv_prev = None
            for j in range(NFF):
                bv_j = ffn_a(j)
                if j >= 1:
                    ffn_b(j - 1, bv_prev)
                bv_prev = bv_j
            ffn_b(NFF - 1, bv_prev)

            if g + 1 < NTT:
                norm_stats(g + 1, 0)
            proj_residual(74, NFF, gT, "gT", 1, dtmp, lambda tb: ["dtmp%d" % tb])
            if g + 1 < NTT:
                norm_trans((g + 1) // NT, am, modT[:, 0, :, :])

            for s in range(4):
                sl = (g % 2) * 4 + s
                ybi = s % 2
                ssf = pst[:, 30:31]
                rsf = pst[:, 31:32]
                A("act", "activation", out=fjunk, in_=xbuf[:, sl, :], func=AF.Square, accum_out=ssf,
                  reads=["x%d" % sl], writes=["fjunk", "ssf%d" % par])
                A("pool", "tensor_scalar", out=ssf, in0=ssf, scalar1=1.0 / D, scalar2=EPS, op0=ALU.mult, op1=ALU.add,
                  reads=["ssf%d" % par], writes=["ssf%d" % par])
                A("pool", "tensor_tensor", out=rsf, in0=ssf, in1=neghalf[:, 0:1], op=ALU.pow,
                  reads=["ssf%d" % par, "neghalf"], writes=["rsf%d" % par])
                A("dve", "scalar_tensor_tensor", out=yb[:, ybi, :], in0=xbuf[:, sl, :], scalar=rsf,
                  in1=rowbc[:, RF_FG:RF_FG + D], op0=ALU.mult, op1=ALU.mult,
                  reads=["x%d" % sl, "rsf%d" % par, "rowbc"], writes=["yb%d" % ybi])
                r0 = g * 512 + s * 128
                A("pool", "dma_start", out=yout_d[r0:r0 + 128, :], in_=yb[:, ybi, :], reads=["yb%d" % ybi],
                  dma="yo%d" % ybi)
        P.barrier()

        nw = P.emit(lambda e, ep: sems[(e, ep)], dsem)
        if dbg:
            print("ops", len(P.ops), "waits", nw, "maxcnt", P.maxcnt, "arena mixer", mixer_top, "ffn", FA.off, "pro", PA.off)
    return nc


def _weight_units(w_in, w_out, w_up, w_down):
    U = np.empty((NUNITS, 128, 1024), np.float32)

    def fm_unit(W, cols):
        return W[:, cols].reshape(8, 128, 128).transpose(1, 0, 2).reshape(128, 1024)

    def tm_unit(W, k0, cols):
        return W[k0 * 128:(k0 + 2) * 128][:, cols].reshape(2, 128, 512).transpose(1, 0, 2).reshape(128, 1024)

    ar = np.arange
    u = 0
    for c in range(4):
        U[u] = fm_unit(w_in, 1280 + c * 128 + ar(128)); u += 1
    for c in range(4):
        U[u] = fm_unit(w_in, 768 + c * 128 + ar(128)); u += 1
    for c in range(4):
        U[u] = fm_unit(w_in, 2304 + c * 128 + ar(128)); u += 1
    U[u] = fm_unit(w_in, 512 + ar(128)); u += 1
    for gq in range(4):
        cols = np.concatenate([gq * 64 + ar(64), (4 + gq) * 64 + ar(64)])
        U[u] = fm_unit(w_in, cols); u += 1
    for k in range(4):
        U[u] = tm_unit(w_in, 2 * k, 1792 + ar(512)); u += 1
    U[u] = fm_unit(w_in, 640 + ar(128)); u += 1
    assert u == 22
    for nh in range(2):
        for k in range(4):
            U[u] = tm_unit(w_out, 2 * k, nh * 512 + ar(512)); u += 1
    assert u == 30
    for j in range(NFF):
        U[u] = fm_unit(w_up, j * 128 + ar(128)); u += 1
        U[u] = fm_unit(w_up, DFF + j * 128 + ar(128)); u += 1
    assert u == 74
    for nh in range(2):
        for jp in range(11):
            U[u] = tm_unit(w_down, 2 * jp, nh * 512 + ar(512)); u += 1
    assert u == NUNITS
    return np.ascontiguousarray(U[UORDER]).reshape(NUNITS * 128, 1024)


def _host_prep(inp, T, NSEQ, n_cores):
    f = lambda a: np.ascontiguousarray(np.asarray(a, dtype=np.float32))
    w_in, w_out, w_up, w_down = f(inp["w_in"])[0], f(inp["w_out"])[0], f(inp["w_up"])[0], f(inp["w_down"])[0]
    wun = _weight_units(w_in, w_out, w_up, w_down)
    ada_w = f(inp["ada_w"])[0]
    adaun = np.ascontiguousarray(ada_w.reshape(8, 128, 12, 512).transpose(2, 1, 0, 3))
    adab = f(inp["ada_b"])[0]
    b_in = f(inp["b_in"])[0]
    sF = np.zeros((128, NSF), np.float32)
    fm_cols = ([1280 + c * 128 for c in range(4)] + [768 + c * 128 for c in range(4)] +
               [2304 + c * 128 for c in range(4)] + [512])
    for i, c0 in enumerate(fm_cols):
        sF[:, SF_BFM + i] = b_in[c0:c0 + 128]
    for gq in range(4):
        sF[0:64, SF_BFM + 13 + gq] = b_in[gq * 64:(gq + 1) * 64]
        sF[64:128, SF_BFM + 13 + gq] = b_in[(4 + gq) * 64:(5 + gq) * 64]
    sF[:, SF_GMIX:SF_GMIX + 8] = f(inp["mix_norm_g"])[0].reshape(8, 128).T
    sF[:, SF_GFFN:SF_GFFN + 8] = f(inp["ffn_norm_g"])[0].reshape(8, 128).T
    sF[:, SF_GA:SF_GA + 4] = f(inp["attn_out_g"])[0].reshape(4, 128).T
    sF[:, SF_GH:SF_GH + 4] = f(inp["hgrn_out_g"])[0].reshape(4, 128).T
    cw = f(inp["conv_w"])[0]
    for k in range(3):
        sF[:, SF_CW + k * NFF:SF_CW + (k + 1) * NFF] = cw[k].reshape(NFF, 128).T
    sF[:, SF_CB:SF_CB + NFF] = f(inp["conv_b"])[0].reshape(NFF, 128).T
    lbl = f(inp["hgrn_lb_logits"])
    sF[:, SF_L0:SF_L0 + 4] = lbl[0].reshape(4, 128).T
    sF[:, SF_L1:SF_L1 + 4] = lbl[1].reshape(4, 128).T
    rowF = np.zeros(NROW, np.float32)
    rowF[RF_FG:RF_FG + D] = f(inp["final_norm_g"])
    rowF[RF_BIH:RF_BIH + 512] = b_in[1792:2304]
    rowF[RF_BVA:RF_BVA + 128] = b_in[640:768]
    rowF[RF_SINK:RF_SINK + 8] = f(inp["attn_sinks"])[0]
    x = f(inp["x"])
    c = f(inp["c"])
    maps = []
    for i in range(n_cores):
        xs = x[i * NSEQ:(i + 1) * NSEQ].reshape(NSEQ * T, D)
        cs = c[i * NSEQ:(i + 1) * NSEQ]
        cTm = np.ascontiguousarray(cs.reshape(NSEQ, 8, 128).transpose(2, 1, 0).reshape(128, 8 * NSEQ))
        maps.append({"xin": np.ascontiguousarray(xs), "wun": wun, "adaun": adaun, "adab": adab, "cT": cTm,
                     "smallF": sF, "rowF": rowF})
    return maps


_NC_CACHE = {}


def kernel(**inputs):
    x = np.asarray(inputs["x"])
    B, T, _ = x.shape
    n_cores = 8 if B % 8 == 0 else 1
    NSEQ = B // n_cores
    key = (T, NSEQ)
    if key not in _NC_CACHE:
        _NC_CACHE[key] = build_nc(T, NSEQ)
    nc = _NC_CACHE[key]
    maps = _host_prep(inputs, T, NSEQ, n_cores)
    res = run_bass_kernel_spmd(nc, maps, core_ids=list(range(n_cores)))
    out = np.stack([np.asarray(r["y"]).reshape(NSEQ, T, D) for r in res.results], axis=0)
    return out.reshape(B, T, D).astype(np.float32)
```
